# Optimizing a Trainium2 kernel written in Bass

```python
import jax, jax.numpy as jnp
from jax import lax
import numpy as np

D_MODEL = 2048
BATCH = 4
SEQ = 8192
DEPTH = 1
DEC_BATCH = 32
DEC_SEQ = 16
PAST_LEN = 2048

CHUNK = 64
N_LEFT_CHUNKS = 8
KV_WINDOW = N_LEFT_CHUNKS * CHUNK
BAND = KV_WINDOW + CHUNK
D_ATTN = D_MODEL // 2
N_HEADS_A = 8
HEAD_DIM = D_ATTN // N_HEADS_A
MAX_REL = 256
D_SGU = D_MODEL // 2
N_GROUPS_B = 8
GROUP_DIM_B = D_SGU // N_GROUPS_B
SGU_CHUNK = 128
D_FF = 5632
CONV_W = 3
EPS = 1e-6
SPLITS = (D_ATTN, 2 * D_ATTN, 3 * D_ATTN, 3 * D_ATTN + D_SGU, 3 * D_ATTN + 2 * D_SGU,
          3 * D_ATTN + 2 * D_SGU + D_MODEL)
D_IN = 3 * D_ATTN + 2 * D_SGU + 2 * D_MODEL

kernel_name = "hybrid_chunk_attn_sgu_convffn_step"


def rms_norm(x, g):
    x32 = x.astype(jnp.float32)
    inv = lax.rsqrt(jnp.mean(x32 * x32, axis=-1, keepdims=True) + EPS)
    return (x32 * inv).astype(x.dtype) * g


def mixer_inputs(x, norm_g, w_in, sgu_norm_g):
    B, T, _ = x.shape
    xn = rms_norm(x, norm_g)
    h = xn @ w_in
    q, k, v, u, vb, ga, gb = jnp.split(h, SPLITS, axis=-1)
    heads = lambda t: t.reshape(B, T, N_HEADS_A, HEAD_DIM)
    u = jax.nn.gelu(u, approximate=False)
    vb = rms_norm(jax.nn.gelu(vb, approximate=False), sgu_norm_g)
    return heads(q), heads(k), heads(v), u, vb, ga, gb


def band_attention(q, k, v, bias, mask):
    s = jnp.einsum('bqhd,bkhd->bhqk', q, k).astype(jnp.float32) * (HEAD_DIM ** -0.5)
    s = jnp.where(mask, s + bias.astype(jnp.float32), -1e30)
    p = jax.nn.softmax(s, axis=-1).astype(v.dtype)
    return jnp.einsum('bhqk,bkhd->bqhd', p, v)


def rel_bias_lookup(rel_bias, dist):
    return rel_bias[:, jnp.clip(dist, -MAX_REL, MAX_REL) + MAX_REL]


def chunk_attention_prompt(q, k, v, rel_bias):
    B, T, H, Dh = q.shape
    nc = T // CHUNK
    pad = ((0, 0), (KV_WINDOW, 0), (0, 0), (0, 0))
    kp, vp = jnp.pad(k, pad), jnp.pad(v, pad)
    i = jnp.arange(CHUNK)[:, None]
    j = jnp.arange(BAND)[None, :]
    bias = rel_bias_lookup(rel_bias, i - j + KV_WINDOW)
    qc = q.reshape(B, nc, CHUNK, H, Dh).transpose(1, 0, 2, 3, 4)

    def one_chunk(args):
        c, q_blk = args
        start = c * CHUNK
        k_band = lax.dynamic_slice_in_dim(kp, start, BAND, axis=1)
        v_band = lax.dynamic_slice_in_dim(vp, start, BAND, axis=1)
        k_pos = start - KV_WINDOW + jnp.arange(BAND)
        mask = jnp.broadcast_to((k_pos >= 0)[None, :], (CHUNK, BAND))
        return band_attention(q_blk, k_band, v_band, bias, mask)

    out = lax.map(one_chunk, (jnp.arange(nc), qc))
    return out.transpose(1, 0, 2, 3, 4).reshape(B, T, H * Dh)


def chunk_attention_sample(q, k_new, v_new, k_cache, v_cache, rel_bias):
    B, S, H, Dh = q.shape
    L = k_cache.shape[1]
    k = jnp.concatenate([k_cache, k_new], axis=1)
    v = jnp.concatenate([v_cache, v_new], axis=1)
    q_pos = PAST_LEN + jnp.arange(S)
    k_pos = jnp.concatenate([PAST_LEN - L + jnp.arange(L), PAST_LEN + jnp.arange(S)])
    cq = q_pos[:, None] // CHUNK
    ck = k_pos[None, :] // CHUNK
    mask = (ck <= cq) & (cq - ck <= N_LEFT_CHUNKS)
    bias = rel_bias_lookup(rel_bias, q_pos[:, None] - k_pos[None, :])
    return band_attention(q, k, v, bias, mask).reshape(B, S, H * Dh)


def causal_sgu_weights(w_s):
    return w_s * jnp.tril(jnp.ones((SGU_CHUNK, SGU_CHUNK), w_s.dtype))


def sgu_prompt(u, vb, w_s, b_s):
    B, T, _ = vb.shape
    nc = T // SGU_CHUNK
    vg = vb.reshape(B, nc, SGU_CHUNK, N_GROUPS_B, GROUP_DIM_B)
    mixed = jnp.einsum('gij,bcjgd->bcigd', causal_sgu_weights(w_s), vg)
    mixed = mixed + b_s.T[None, None, :, :, None]
    return u * mixed.reshape(B, T, D_SGU)


def sgu_sample(u, vb, w_s, b_s):
    B, S, _ = vb.shape
    vg = vb.reshape(B, S, N_GROUPS_B, GROUP_DIM_B)
    w = causal_sgu_weights(w_s)[:, :S, :S]
    mixed = jnp.einsum('gij,bjgd->bigd', w, vg) + b_s[:, :S].T[None, :, :, None]
    return u * mixed.reshape(B, S, D_SGU)


def merge_branches(x, a, s, ga, gb, w_branch_a, w_branch_b, w_out):
    m = jax.nn.sigmoid(ga) * (a @ w_branch_a) + jax.nn.sigmoid(gb) * (s @ w_branch_b)
    return x + m @ w_out


def conv_ffn(x, h_hist, norm_g, w_up, conv_w, conv_b, w_down):
    T = x.shape[1]
    h = rms_norm(x, norm_g) @ w_up
    h_ext = jnp.concatenate([h_hist, h], axis=1)
    hc = conv_b + sum(conv_w[t] * h_ext[:, t:t + T] for t in range(CONV_W))
    gate, val = jnp.split(hc, 2, axis=-1)
    y = x + (jax.nn.gelu(gate, approximate=False) * val) @ w_down
    return y, h_ext[:, -(CONV_W - 1):]


def layer_prompt(x, norm_mix_g, w_in, rel_bias, sgu_norm_g, w_s, b_s, w_branch_a, w_branch_b,
                 w_out, norm_ffn_g, w_up, conv_w, conv_b, w_down):
    B, T, _ = x.shape
    q, k, v, u, vb, ga, gb = mixer_inputs(x, norm_mix_g, w_in, sgu_norm_g)
    a = chunk_attention_prompt(q, k, v, rel_bias)
    s = sgu_prompt(u, vb, w_s, b_s)
    x = merge_branches(x, a, s, ga, gb, w_branch_a, w_branch_b, w_out)
    h_hist = jnp.zeros((B, CONV_W - 1, 2 * D_FF), x.dtype)
    x, conv_state = conv_ffn(x, h_hist, norm_ffn_g, w_up, conv_w, conv_b, w_down)
    keep = min(KV_WINDOW, T)
    return x, k[:, T - keep:], v[:, T - keep:], conv_state


def layer_sample(x, k_cache, v_cache, conv_cache, norm_mix_g, w_in, rel_bias, sgu_norm_g, w_s, b_s,
                 w_branch_a, w_branch_b, w_out, norm_ffn_g, w_up, conv_w, conv_b, w_down):
    q, k, v, u, vb, ga, gb = mixer_inputs(x, norm_mix_g, w_in, sgu_norm_g)
    a = chunk_attention_sample(q, k, v, k_cache, v_cache, rel_bias)
    s = sgu_sample(u, vb, w_s, b_s)
    x = merge_branches(x, a, s, ga, gb, w_branch_a, w_branch_b, w_out)
    x, conv_state = conv_ffn(x, conv_cache, norm_ffn_g, w_up, conv_w, conv_b, w_down)
    return x, k, v, vb, conv_state


def setup_inputs(seed: int = 0) -> dict:
    key = jax.random.key(seed)
    ks = jax.random.split(key, 24)
    f32 = jnp.float32
    nrm = lambda k, shape, scale: jax.random.normal(k, shape, f32) * scale
    L = min(KV_WINDOW, PAST_LEN)
    return {
        "x_prompt": nrm(ks[0], (BATCH, SEQ, D_MODEL), 1.0),
        "x_sample": nrm(ks[1], (DEC_BATCH, DEC_SEQ, D_MODEL), 1.0),
        "cache_k": nrm(ks[2], (DEPTH, DEC_BATCH, L, N_HEADS_A, HEAD_DIM), 1.0),
        "cache_v": nrm(ks[3], (DEPTH, DEC_BATCH, L, N_HEADS_A, HEAD_DIM), 1.0),
        "cache_ffn_conv": nrm(ks[4], (DEPTH, DEC_BATCH, CONV_W - 1, 2 * D_FF), 1.0),
        "norm_mix_g": 1.0 + nrm(ks[5], (DEPTH, D_MODEL), 0.02),
        "w_in": nrm(ks[6], (DEPTH, D_MODEL, D_IN), D_MODEL ** -0.5),
        "rel_bias": nrm(ks[7], (DEPTH, N_HEADS_A, 2 * MAX_REL + 1), 0.5),
        "sgu_norm_g": 1.0 + nrm(ks[8], (DEPTH, D_SGU), 0.02),
        "w_s": nrm(ks[9], (DEPTH, N_GROUPS_B, SGU_CHUNK, SGU_CHUNK), SGU_CHUNK ** -0.5),
        "b_s": 1.0 + nrm(ks[10], (DEPTH, N_GROUPS_B, SGU_CHUNK), 0.1),
        "w_branch_a": nrm(ks[11], (DEPTH, D_ATTN, D_MODEL), D_ATTN ** -0.5),
        "w_branch_b": nrm(ks[12], (DEPTH, D_SGU, D_MODEL), D_SGU ** -0.5),
        "w_out": nrm(ks[13], (DEPTH, D_MODEL, D_MODEL), D_MODEL ** -0.5),
        "norm_ffn_g": 1.0 + nrm(ks[14], (DEPTH, D_MODEL), 0.02),
        "w_up": nrm(ks[15], (DEPTH, D_MODEL, 2 * D_FF), D_MODEL ** -0.5),
        "conv_w": nrm(ks[16], (DEPTH, CONV_W, 2 * D_FF), CONV_W ** -0.5),
        "conv_b": nrm(ks[17], (DEPTH, 2 * D_FF), 0.01),
        "w_down": nrm(ks[18], (DEPTH, D_FF, D_MODEL), D_FF ** -0.5),
        "norm_final_g": 1.0 + nrm(ks[19], (D_MODEL,), 0.02),
    }


def reference(x_prompt, x_sample, cache_k, cache_v, cache_ffn_conv, norm_mix_g, w_in, rel_bias,
              sgu_norm_g, w_s, b_s, w_branch_a, w_branch_b, w_out, norm_ffn_g, w_up, conv_w, conv_b,
              w_down, norm_final_g):
    yp, ys = x_prompt, x_sample
    nk_p, nv_p, nc_p, nk_s, nv_s, nvb_s, nc_s = [], [], [], [], [], [], []
    for l in range(DEPTH):
        lp = (norm_mix_g[l], w_in[l], rel_bias[l], sgu_norm_g[l], w_s[l], b_s[l], w_branch_a[l],
              w_branch_b[l], w_out[l], norm_ffn_g[l], w_up[l], conv_w[l], conv_b[l], w_down[l])
        yp, kp_, vp_, cp_ = layer_prompt(yp, *lp)
        ys, ks_, vs_, vbs_, cs_ = layer_sample(ys, cache_k[l], cache_v[l], cache_ffn_conv[l], *lp)
        nk_p.append(kp_); nv_p.append(vp_); nc_p.append(cp_)
        nk_s.append(ks_); nv_s.append(vs_); nvb_s.append(vbs_); nc_s.append(cs_)
    y_prompt = rms_norm(yp, norm_final_g)
    y_sample = rms_norm(ys, norm_final_g)
    new_k_prompt = jnp.stack(nk_p)
    new_v_prompt = jnp.stack(nv_p)
    new_k_sample = jnp.stack(nk_s)
    new_v_sample = jnp.stack(nv_s)
    new_sgu_v_sample = jnp.stack(nvb_s)
    new_conv_prompt = jnp.stack(nc_p)
    new_conv_sample = jnp.stack(nc_s)
    return (y_prompt, y_sample, new_k_prompt, new_v_prompt, new_k_sample, new_v_sample,
            new_sgu_v_sample, new_conv_prompt, new_conv_sample)
```

```python
import numpy as np
import concourse.bass as bass
import concourse.mybir as mybir
from concourse.bass_utils import run_bass_kernel_spmd

F32 = mybir.dt.float32
BF16 = mybir.dt.bfloat16
AF = mybir.ActivationFunctionType
ALU = mybir.AluOpType

D = 2048
DIN = 9216
DFF = 5632
NH = 8
HALO = 640
NTOK = 4096
EPS = 1e-6
NEG = -30000.0


class _Op:
    __slots__ = ("eng", "fn", "idx", "deps", "signal", "sig_val", "dma_key", "dma_val", "waits", "gid", "total")


class Sched:
    ENGS = ("pe", "act", "dve", "pool", "sp")

    def __init__(self, nc):
        self.nc = nc
        self.ops = {e: [] for e in self.ENGS}
        self.lastw = {}
        self.readers = {}
        self.dma_cnt = {}
        self.nops = 0

    def _add(self, eng, fn, reads, writes):
        op = _Op()
        op.eng = eng
        op.fn = fn
        op.signal = False
        op.sig_val = 0
        op.dma_key = None
        op.dma_val = 0
        op.total = False
        op.gid = self.nops
        self.nops += 1
        deps = {}
        lw = self.lastw
        rd = self.readers
        for k in reads:
            w = lw.get(k)
            if w is not None:
                deps[w.gid] = w
        for k in writes:
            w = lw.get(k)
            if w is not None:
                deps[w.gid] = w
            for r in rd.get(k, ()):
                deps[r.gid] = r
        inorder = eng in ("pe", "act", "dve", "pool")
        for k in reads:
            lst = rd.setdefault(k, [])
            if inorder and not getattr(self, "_dma_building", False):
                for i_, r in enumerate(lst):
                    if r.eng == eng and r.dma_key is None:
                        lst[i_] = op
                        break
                else:
                    lst.append(op)
            else:
                lst.append(op)
        for k in writes:
            lw[k] = op
            rd[k] = []
        deps.pop(op.gid, None)
        op.deps = list(deps.values())
        op.idx = len(self.ops[eng])
        self.ops[eng].append(op)
        return op

    def op(self, eng, fn, reads=(), writes=()):
        return self._add(eng, fn, reads, writes)

    def dma(self, eng, out, in_, reads=(), writes=(), sem=None, total=False):
        self._dma_building = True
        op = self._add(eng, lambda e, out=out, in_=in_: e.dma_start(out=out, in_=in_), reads, writes)
        self._dma_building = False
        op.dma_key = sem
        op.total = total
        c = self.dma_cnt.get(sem, 0) + 1
        self.dma_cnt[sem] = c
        op.dma_val = 16 * c
        return op

    def finish(self, same_eng_dist=4):
        nc = self.nc
        for e in self.ENGS:
            for op in self.ops[e]:
                need = []
                for d in op.deps:
                    if d.dma_key is not None:
                        need.append(d)
                    elif d.eng == op.eng:
                        if op.dma_key is not None:
                            d.signal = True
                            need.append(d)
                        elif e == "pe":
                            continue
                        elif op.idx - d.idx <= same_eng_dist:
                            d.signal = True
                            need.append(d)
                    else:
                        d.signal = True
                        need.append(d)
                op.deps = need
        eng_sem = {}
        for e in self.ENGS:
            cnt = 0
            for op in self.ops[e]:
                if op.signal:
                    cnt += 1
                    op.sig_val = cnt
                if op.dma_key is not None and op.total:
                    op.dma_val = 16 * self.dma_cnt[op.dma_key]
            eng_sem[e] = nc.alloc_semaphore(f"s_{e}")
            print(f"[sched] {e}: {len(self.ops[e])} ops, {cnt} signals", flush=True)
        dma_sem = {k: nc.alloc_semaphore(f"d_{i}") for i, k in enumerate(self.dma_cnt)}
        print(f"[sched] {len(dma_sem)} dma sems", flush=True)
        for e in self.ENGS:
            waited = {}
            for op in self.ops[e]:
                req = {}
                for d in op.deps:
                    if d.dma_key is not None:
                        key = ("d", d.dma_key)
                        sem = dma_sem[d.dma_key]
                        val = d.dma_val
                    else:
                        key = ("e", d.eng)
                        sem = eng_sem[d.eng]
                        val = d.sig_val
                    if val > req.get(key, (None, 0))[1]:
                        req[key] = (sem, val)
                w = []
                for key, (sem, val) in req.items():
                    if val > waited.get(key, 0):
                        waited[key] = val
                        w.append((sem, val))
                op.waits = w
        engobj = {"pe": "tensor", "act": "scalar", "dve": "vector", "pool": "gpsimd", "sp": "sync"}
        final_waits = [(dma_sem[k], 16 * c) for k, c in self.dma_cnt.items()]

        def run(e, eng):
            es = eng_sem[e]
            for op in self.ops[e]:
                for sem, val in op.waits:
                    eng.wait_ge(sem, val)
                ins = op.fn(eng)
                if op.signal:
                    ins.then_inc(es, 1)
                if op.dma_key is not None:
                    ins.then_inc(dma_sem[op.dma_key], 16)
            if e == "sp":
                for sem, val in final_waits:
                    eng.wait_ge(sem, val)

        with nc.Block() as block:
            for e in self.ENGS:
                if not self.ops[e] and e != "sp":
                    continue
                getattr(block, engobj[e])(lambda eng, e=e: run(e, eng))


def build(n_main=8, do_halo=True, do_sample=True):
    nc = bass.Bass("TRN2", target_bir_lowering=False)
    S = Sched(nc)

    def din(name, shape, dt=F32):
        return nc.dram_tensor(name, list(shape), dt, kind="ExternalInput").ap()

    def dout(name, shape):
        return nc.dram_tensor(name, list(shape), F32, kind="ExternalOutput").ap()

    def dscr(name, shape):
        return nc.dram_tensor(name, list(shape), BF16, kind="Internal").ap()

    xT_d = din("xT", [D, HALO + NTOK])
    xsT_d = din("xsT", [D, 64])
    ckT_d = din("ckT", [4, NH, 128, 512])
    cv_d = din("cv", [4, 512, 1024])
    cconvT_d = din("cconvT", [2 * DFF, 8])
    w_in_d = din("w_in", [D, DIN])
    w_a_d = din("w_a", [1024, D])
    w_b_d = din("w_b", [1024, D])
    w_out_d = din("w_out", [D, D])
    w_up_d = din("w_up", [D, 2 * DFF])
    w_down_d = din("w_down", [DFF, D])
    gcols_d = din("gcols", [128, 48])
    gsgu_d = din("gsgu", [1024])
    convp_d = din("convp", [128, 88 * 4])
    wsT_d = din("wsT", [NH, 128, 128])
    bs_d = din("bs", [1024])
    tbg_d = din("tbg", [NH, 128, 640])
    mk_d = din("mk", [128, 640])
    ident_d = din("ident", [128, 128])
    hones_d = din("hones", [128, 128])
    triu_d = din("triu", [128, 128])
    maskbd_d = din("maskbd", [64, 64])

    w_in_s = dscr("w_in_s", [D, DIN])
    w_a_s = dscr("w_a_s", [1024, D])
    w_b_s = dscr("w_b_s", [1024, D])
    w_out_s = dscr("w_out_s", [D, D])
    w_up_s = dscr("w_up_s", [D, 2 * DFF])
    w_down_s = dscr("w_down_s", [DFF, D])

    NBLK = 144
    wscr = dscr("wscr", [NBLK, 128, 4096])

    y_d = dout("y", [NTOK, D])
    ys_d = dout("ys", [64, D])
    nk_d = dout("nk", [512, 1024])
    nv_d = dout("nv", [512, 1024])
    nks_d = dout("nks", [64, 1024])
    nvs_d = dout("nvs", [64, 1024])
    nvbs_d = dout("nvbs", [64, 1024])
    ncp_d = dout("ncp", [2, 2 * DFF])
    ncs_d = dout("ncs", [4, 2, 2 * DFF])

    sb = nc.alloc_sbuf_tensor
    xT = sb("xT_sb", [128, 16, 512], F32)
    xn = sb("xn_sb", [128, 16, 512], BF16)
    KT = sb("KT_sb", [128, NH, 1024], BF16)
    VR = sb("VR_sb", [128, 8, 1024], BF16)
    act = sb("act_sb", [128, 32, 512], BF16)
    TB = sb("TB_sb", [128, NH, 640], F32)
    NTMP = 6
    tmpf = sb("tmpf_sb", [128, NTMP, 514], F32)
    NPB = 3
    pbf = sb("pbf_sb", [128, NPB, 640], BF16)
    NWB = 4
    wbuf = sb("wbuf_sb", [128, NWB, 4096], BF16)
    rinv = sb("rinv_sb", [128, 512], F32)
    gcols = sb("gcols_sb", [128, 48], F32)
    gsgu = sb("gsgu_sb", [128, 1024], F32)
    convp = sb("convp_sb", [128, 88, 4], F32)
    wsT = sb("wsT_sb", [128, NH, 128], BF16)
    wbd = sb("wbd_sb", [64, NH, 64], BF16)
    BS = sb("BS_sb", [128, NH, 128], F32)
    ident = sb("ident_sb", [128, 128], F32)
    onesb = sb("onesb_sb", [128, 128], BF16)
    honesb = sb("honesb_sb", [128, 128], BF16)
    onesD = sb("onesD_sb", [128, 128], BF16)
    hist = sb("hist_sb", [128, 88, 2], F32)
    hs_hist = sb("hs_hist_sb", [128, 88, 4, 2], F32)
    hs_last = sb("hs_last_sb", [128, 88, 4, 2], F32)
    kTs = sb("kTs_sb", [128, NH, 64], BF16)
    cols = sb("cols_sb", [128, 8], F32)
    xh2 = sb("xh2_sb", [128, 16, 2], BF16)
    print("[build] sbuf bytes remaining", nc.sbuf_bytes_remaining, flush=True)

    pmm = [nc.alloc_psum_tensor(f"pmm{i}", [128, 512], F32) for i in range(4)]
    pS = nc.alloc_psum_tensor("pS", [128, 1024], F32)
    pO = nc.alloc_psum_tensor("pO", [128, 512], F32)
    pD = nc.alloc_psum_tensor("pD", [128, 512], F32)

    st = {"mm": 0, "tmp": 0, "pb": 0, "wb": 0, "ev": 0, "fo": 0}

    resv = set()

    def next_mm():
        while True:
            i = st["mm"] % 4
            st["mm"] += 1
            if i not in resv:
                return i

    def next_tmp():
        i = st["tmp"] % NTMP
        st["tmp"] += 1
        return i

    def next_pb():
        i = st["pb"] % NPB
        st["pb"] += 1
        return i

    def ev_eng():
        st["ev"] += 1
        return "act" if st["ev"] % 2 else "dve"

    def setup_load(dst, src, key):
        S.dma("sp", dst, src, reads=(), writes=[key], sem="setup", total=True)

    setup_load(gcols[:], gcols_d, "gcols")
    setup_load(gsgu[:], gsgu_d.partition_broadcast(128), "gsgu")
    setup_load(convp[:], convp_d.rearrange("p (c f) -> p c f", f=4), "convp")
    setup_load(BS[:], bs_d.partition_broadcast(128).rearrange("p (g i) -> p g i", i=128), "BS")
    setup_load(ident[:], ident_d, "ident")
    setup_load(TB[:], tbg_d.rearrange("h r m -> r h m"), "TB")
    S.op("pool", lambda e: e.memset(onesb[:], 1.0), writes=["onesb"])
    S.op("pool", lambda e: e.memset(onesD[:], 1.0 / D), writes=["onesD"])
    S.op("pool", lambda e: e.memset(hist[:], 0.0), writes=[("hist", c) for c in range(88)])
    S.op("pool", lambda e: e.memset(wbd[:], 0.0), writes=["wbd0"])
    setup_load(tmpf[:, 0, 0:128], hones_d, ("tmpf", 0))
    S.op("act", lambda e: e.copy(honesb[:], tmpf[:, 0, 0:128]), reads=[("tmpf", 0)], writes=["honesb"])
    setup_load(tmpf[:, 1, 0:512], mk_d[:, 0:512], ("tmpf", 1))
    setup_load(tmpf[:, 2, 0:128], mk_d[:, 512:640], ("tmpf", 2))
    for h in range(NH):
        S.op("dve", lambda e, h=h: e.tensor_tensor(TB[:, h, 0:512], TB[:, h, 0:512], tmpf[:, 1, 0:512], ALU.add),
             reads=["TB", ("tmpf", 1)], writes=["TB"])
        S.op("dve", lambda e, h=h: e.tensor_tensor(TB[:, h, 512:640], TB[:, h, 512:640], tmpf[:, 2, 0:128], ALU.add),
             reads=["TB", ("tmpf", 2)], writes=["TB"])
    setup_load(tmpf[:, 3, 0:128], triu_d, ("tmpf", 3))
    for g in range(NH):
        t = 4 + (g % 2)
        S.dma("sp", tmpf[:, t, 0:128], wsT_d[g], reads=(), writes=[("tmpf", t)], sem=("tmpf", t))
        S.op("dve", lambda e, g=g, t=t: e.tensor_tensor(wsT[:, g, :], tmpf[:, t, 0:128], tmpf[:, 3, 0:128], ALU.mult),
             reads=[("tmpf", t), ("tmpf", 3)], writes=["wsT"])
    if do_sample:
        for s_ in range(4):
            S.dma("sp", wbd[16 * s_:16 * s_ + 16, :, 16 * s_:16 * s_ + 16], wsT[0:16, :, 0:16],
                  reads=["wsT", "wbd0"], writes=[("wbdl", s_)], sem="wbdl", total=True)

    wtag = {id(w_in_s): ("in", w_in_d), id(w_a_s): ("a", w_a_d), id(w_b_s): ("b", w_b_d), id(w_out_s): ("out", w_out_d),
            id(w_up_s): ("up", w_up_d), id(w_down_s): ("down", w_down_d)}
    scr_done = set()
    scr_idx = {}
    scr_touch = {}
    scr_wt = {}
    WB_SPREAD = 1

    def wload(scr, castkey, k0, k1, c0, c1):
        i = st["wb"] % NWB
        st["wb"] += 1
        nk = k1 - k0
        ncol = c1 - c0
        assert nk * ncol <= 4096
        view = wbuf[:, i, 0:nk * ncol].rearrange("p (k c) -> p k c", c=ncol)
        flat = wbuf[:, i, 0:nk * ncol]
        tag, w32 = wtag[id(scr)]
        rk = ("scr", tag, k0, k1, c0, c1)
        if rk in scr_done:
            S.dma("sp", flat, wscr[scr_idx[rk], :, 0:nk * ncol], reads=[rk], writes=[("wb", i)], sem=("wb", i))
        else:
            scr_idx[rk] = len(scr_idx)
            assert scr_idx[rk] < NBLK
            src32 = w32.rearrange("(k p) c -> p k c", p=128)[:, k0:k1, c0:c1]
            S.dma("pool", view, src32, reads=(), writes=[("wb", i)], sem=("wbc", i))
            scr_done.add(rk)
            S.dma("sp", wscr[scr_idx[rk], :, 0:nk * ncol], flat, reads=[("wb", i)], writes=[rk], sem=("wbst", i))
        return view, ("wb", i)

    def in_castkey(c0):
        return "in0" if c0 < 3072 else ("in1" if c0 < 5120 else "in2")

    class Stats:
        def __init__(self, T, lag=2, sq_eng="act"):
            self.sq_eng = sq_eng
            self.T = T
            self.b = next_mm()
            resv.add(self.b)
            self.n = 0
            self.pend = []
            self.lag = lag

        def chunk(self, k):
            T, b = self.T, self.b
            p = next_pb()
            if self.sq_eng == "act":
                S.op("act", lambda e, k=k, p=p: e.activation(pbf[:, p, 0:T], xT[:, k, 0:T], AF.Square),
                     reads=[("x", k)], writes=[("pb", p)])
            else:
                S.op("dve", lambda e, k=k, p=p: e.tensor_tensor(pbf[:, p, 0:T], xT[:, k, 0:T], xT[:, k, 0:T], ALU.mult),
                     reads=[("x", k)], writes=[("pb", p)])
            self.pend.append(p)
            while len(self.pend) > self.lag:
                self._mm()

        def _mm(self):
            T, b = self.T, self.b
            p = self.pend.pop(0)
            first = (self.n == 0)
            last = (self.n == 15)
            self.n += 1
            S.op("pe", lambda e, p=p, b=b, first=first, last=last: e.matmul(pmm[b][:, 0:T], onesD[:], pbf[:, p, 0:T],
                                                                           start=first, stop=last, skip_group_check=True),
                 reads=["onesD", ("pb", p)], writes=[("pmm", b)])

        def end(self):
            T, b = self.T, self.b
            while self.pend:
                self._mm()
            assert self.n == 16
            S.op("act", lambda e, b=b: e.activation(rinv[:, 0:T], pmm[b][:, 0:T], AF.Ln, bias=cols[:, 0:1], scale=1.0),
                 reads=[("pmm", b), "cols"], writes=["rinv"])
            S.op("act", lambda e: e.activation(rinv[:, 0:T], rinv[:, 0:T], AF.Exp, scale=-0.5), reads=["rinv"], writes=["rinv"])
            resv.discard(b)

    def rms_stats(T, src_keys):
        stt = Stats(T, lag=2 if NPB >= 3 else 1, sq_eng="dve")
        for k in range(16):
            stt.chunk(k)
        stt.end()

    def normalize_to_xn(T, gbase):
        for k in range(16):
            S.op("dve", lambda e, k=k: e.scalar_tensor_tensor(xn[:, k, 0:T], xT[:, k, 0:T], gcols[:, gbase + k:gbase + k + 1],
                                                              rinv[:, 0:T], ALU.mult, ALU.mult),
                 reads=[("x", k), "gcols", "rinv"], writes=[("xn", k)])

    XN_ALL = [("xn", k) for k in range(16)]

    def proj_fm(T, scr, castkey, kchunks, col0, nchunks, rhs_of, rhs_keys, evac):
        for c2 in range(0, nchunks, 2):
            ncc = min(2, nchunks - c2)
            wv, wk = wload(scr, castkey, 0, kchunks, col0 + c2 * 128, col0 + (c2 + ncc) * 128)
            for cc in range(ncc):
                b = next_mm()
                for k in range(kchunks):
                    S.op("pe", lambda e, k=k, cc=cc, b=b, wv=wv: e.matmul(pmm[b][:, 0:T], wv[:, k, cc * 128:(cc + 1) * 128],
                                                                         rhs_of(k), start=(k == 0), stop=(k == kchunks - 1)),
                         reads=[wk, rhs_keys[k]], writes=[("pmm", b)])
                evac(c2 + cc, b)

    def proj_tm(T, nb, col0, evac, castkey, tok_slices=None):
        if tok_slices is None:
            tok_slices = [(j * 128, 128) for j in range(nb)]
        wl = [wload(w_in_s, castkey, 0, 16, col0 + g * 256, col0 + (g + 1) * 256) for g in range(4)]
        for j, (t0, nt) in enumerate(tok_slices):
            for g in range(4):
                wv, wk = wl[g]
                b = next_mm()
                for k in range(16):
                    S.op("pe", lambda e, k=k, b=b, wv=wv, t0=t0, nt=nt: e.matmul(pmm[b][0:nt, 0:256], xn[:, k, t0:t0 + nt],
                                                                                wv[:, k, :], start=(k == 0), stop=(k == 15)),
                         reads=[wk, ("xn", k)], writes=[("pmm", b)])
                evac(j, g, b)

    def load_x(src_ap_cols, T):
        for k in range(16):
            S.dma("sp", xT[:, k, 0:T], src_ap_cols[k * 128:(k + 1) * 128, :], reads=(), writes=[("x", k)], sem=("x", k))

    def ring_slot(b):
        return b % 8

    def attention_prompt(T, b0, main_tile0, LA=2, heads=None, after_item=None):
        nb = T // 128
        items = []
        for h in (range(NH) if heads is None else heads):
            hitems = []
            for kb in range(b0 - 4, b0 + nb):
                if kb < -5:
                    continue
                c0 = max(2 * kb, 2 * b0)
                c1 = min(2 * kb + 9, 2 * (b0 + nb) - 1)
                if c0 > c1:
                    continue
                hitems.append(dict(h=h, kb=kb, q0=64 * (c0 - 2 * b0), nq=64 * (c1 - c0 + 1), m0=64 * c0 - 128 * kb,
                                   sl=ring_slot(kb), first=False, last=False))
            hitems[0]["first"] = True
            hitems[-1]["last"] = True
            items += hitems

        def front(i, it):
            h, sl, q0, nq, m0 = it["h"], it["sl"], it["q0"], it["nq"], it["m0"]
            sb_ = 512 * (i % 2)
            S.op("pe", lambda e: e.matmul(pS[:, sb_:sb_ + nq], KT[:, h, sl * 128:(sl + 1) * 128], act[:, h, q0:q0 + nq],
                                          start=True, stop=True),
                 reads=[("KT", h, sl), ("act", h)], writes=[("pS", sb_)])
            t = next_tmp()
            S.op("dve", lambda e: e.scalar_tensor_tensor(tmpf[:, t, 0:nq], pS[:, sb_:sb_ + nq], 60.0, TB[:, h, m0:m0 + nq],
                                                         ALU.min, ALU.add),
                 reads=[("pS", sb_), "TB"], writes=[("tmpf", t)])
            p = next_pb()
            S.op("act", lambda e: e.activation(pbf[:, p, 0:nq], tmpf[:, t, 0:nq], AF.Exp),
                 reads=[("tmpf", t)], writes=[("pb", p)])
            it["p"] = p

        def back(it):
            h, sl, q0, nq, p, first, kb = it["h"], it["sl"], it["q0"], it["nq"], it["p"], it["first"], it["kb"]
            po, pd, ko, kd = pO, pD, "pO", "pD"
            S.op("pe", lambda e: e.matmul(po[:, q0:q0 + nq], VR[:, sl, h * 128:(h + 1) * 128], pbf[:, p, 0:nq],
                                          start=first, stop=False, skip_group_check=True),
                 reads=[("VR", sl, h // 2), ("pb", p)], writes=[ko])
            use_h = main_tile0 and kb < 0
            ones_ap = honesb if use_h else onesb
            S.op("pe", lambda e: e.matmul(pd[:, q0:q0 + nq], ones_ap[:], pbf[:, p, 0:nq], start=first, stop=False,
                                          skip_group_check=True),
                 reads=["honesb" if use_h else "onesb", ("pb", p)], writes=[kd])
            if it["last"]:
                t = next_tmp()
                t2 = next_tmp()
                S.op("act", lambda e: e.activation(tmpf[:, t, 0:T], pd[:, 0:T], AF.Ln), reads=[kd], writes=[("tmpf", t)])
                S.op("act", lambda e: e.copy(tmpf[:, t2, 0:T], po[:, 0:T]), reads=[ko], writes=[("tmpf", t2)])
                S.op("act", lambda e: e.activation(tmpf[:, t, 0:T], tmpf[:, t, 0:T], AF.Exp, scale=-1.0),
                     reads=[("tmpf", t)], writes=[("tmpf", t)])
                S.op("pool", lambda e: e.tensor_tensor(act[:, 24 + h, 0:T], tmpf[:, t2, 0:T], tmpf[:, t, 0:T], ALU.mult),
                     reads=[("tmpf", t2), ("tmpf", t)], writes=[("act", 24 + h)])

        n = len(items)
        for i in range(min(LA, n)):
            front(i, items[i])
        for i in range(n):
            back(items[i])
            if i + LA < n:
                front(i + LA, items[i + LA])
            if after_item is not None:
                after_item()

    def sgu_prompt(T):
        nb = T // 128
        for g in range(NH):
            b = next_mm()
            for j in range(nb):
                S.op("pe", lambda e, g=g, j=j, b=b: e.matmul(pmm[b][:, j * 128:(j + 1) * 128],
                                                             act[:, 16 + 2 * j + g // 4, (g % 4) * 128:(g % 4 + 1) * 128],
                                                             wsT[:, g, :], start=True, stop=True),
                     reads=[("act", 16 + 2 * j + g // 4), "wsT"], writes=[("pmm", b)])
            t = next_tmp()
            S.op("dve", lambda e, g=g, b=b, t=t: e.tensor_tensor(
                tmpf[:, t, 0:T].rearrange("p (j i) -> p j i", i=128),
                pmm[b][:, 0:T].rearrange("p (j i) -> p j i", i=128),
                BS[:, g, :].unsqueeze(1).to_broadcast([128, nb, 128]), ALU.add),
                reads=[("pmm", b), "BS"], writes=[("tmpf", t)])
            S.op("dve", lambda e, g=g, t=t: e.tensor_tensor(act[:, g, 0:T], tmpf[:, t, 0:T], act[:, 8 + g, 0:T], ALU.mult),
                 reads=[("tmpf", t), ("act", 8 + g)], writes=[("act", g)])

    def merge(T):
        for f2 in range(0, 16, 2):
            wga, kga = wload(w_in_s, "in2", 0, 16, 5120 + f2 * 128, 5120 + (f2 + 2) * 128)
            wa, ka = wload(w_a_s, "a", 0, 8, f2 * 128, (f2 + 2) * 128)
            wgb, kgb = wload(w_in_s, "in2", 0, 16, 7168 + f2 * 128, 7168 + (f2 + 2) * 128)
            wb_, kb_ = wload(w_b_s, "b", 0, 8, f2 * 128, (f2 + 2) * 128)
            for cc in range(2):
                f = f2 + cc
                b1 = next_mm()
                for k in range(16):
                    S.op("pe", lambda e, k=k, cc=cc, b1=b1, wga=wga: e.matmul(pmm[b1][:, 0:T], wga[:, k, cc * 128:(cc + 1) * 128],
                                                                             xn[:, k, 0:T], start=(k == 0), stop=(k == 15)),
                         reads=[kga, ("xn", k)], writes=[("pmm", b1)])
                t1 = next_tmp()
                S.op("act", lambda e, b1=b1, t1=t1: e.activation(tmpf[:, t1, 0:T], pmm[b1][:, 0:T], AF.Sigmoid),
                     reads=[("pmm", b1)], writes=[("tmpf", t1)])
                b2 = next_mm()
                for k in range(8):
                    S.op("pe", lambda e, k=k, cc=cc, b2=b2, wa=wa: e.matmul(pmm[b2][:, 0:T], wa[:, k, cc * 128:(cc + 1) * 128],
                                                                           act[:, 24 + k, 0:T], start=(k == 0), stop=(k == 7)),
                         reads=[ka, ("act", 24 + k)], writes=[("pmm", b2)])
                S.op("dve", lambda e, b2=b2, t1=t1: e.tensor_tensor(tmpf[:, t1, 0:T], pmm[b2][:, 0:T], tmpf[:, t1, 0:T], ALU.mult),
                     reads=[("pmm", b2), ("tmpf", t1)], writes=[("tmpf", t1)])
                b3 = next_mm()
                for k in range(16):
                    S.op("pe", lambda e, k=k, cc=cc, b3=b3, wgb=wgb: e.matmul(pmm[b3][:, 0:T], wgb[:, k, cc * 128:(cc + 1) * 128],
                                                                             xn[:, k, 0:T], start=(k == 0), stop=(k == 15)),
                         reads=[kgb, ("xn", k)], writes=[("pmm", b3)])
                t2 = next_tmp()
                S.op("act", lambda e, b3=b3, t2=t2: e.activation(tmpf[:, t2, 0:T], pmm[b3][:, 0:T], AF.Sigmoid),
                     reads=[("pmm", b3)], writes=[("tmpf", t2)])
                b4 = next_mm()
                for k in range(8):
                    S.op("pe", lambda e, k=k, cc=cc, b4=b4, wb_=wb_: e.matmul(pmm[b4][:, 0:T], wb_[:, k, cc * 128:(cc + 1) * 128],
                                                                             act[:, k, 0:T], start=(k == 0), stop=(k == 7)),
                         reads=[kb_, ("act", k)], writes=[("pmm", b4)])
                S.op("dve", lambda e, b4=b4, t2=t2: e.tensor_tensor(tmpf[:, t2, 0:T], pmm[b4][:, 0:T], tmpf[:, t2, 0:T], ALU.mult),
                     reads=[("pmm", b4), ("tmpf", t2)], writes=[("tmpf", t2)])
                S.op("dve", lambda e, f=f, t1=t1, t2=t2: e.tensor_tensor(act[:, 8 + f, 0:T], tmpf[:, t1, 0:T], tmpf[:, t2, 0:T], ALU.add),
                     reads=[("tmpf", t1), ("tmpf", t2)], writes=[("act", 8 + f)])

    def out_proj(T):
        stt = Stats(T, lag=2)

        def evac(c, b):
            S.op("dve", lambda e, c=c, b=b: e.tensor_tensor(xT[:, c, 0:T], pmm[b][:, 0:T], xT[:, c, 0:T], ALU.add),
                 reads=[("pmm", b), ("x", c)], writes=[("x", c)])
            stt.chunk(c)
        proj_fm(T, w_out_s, "out", 16, 0, 16, lambda k: act[:, 8 + k, 0:T], [("act", 8 + k) for k in range(16)], evac)
        stt.end()

    def ffn(T, do_down, conv_mode, last2=None, final_stats=False, pre_hist=False):
        passes = [(0, 6), (6, 11)] if do_down else [(0, 11)]
        stt = None
        for (ga0, ga1) in passes:
            nslot = 0
            for ga in range(ga0, ga1):
                for c2 in range(0, 4, 2):
                    wls = []
                    for part in range(2):
                        col0 = (part * 44 + ga * 4) * 128
                        wls.append(wload(w_up_s, "up", 0, 16, col0 + c2 * 128, col0 + (c2 + 2) * 128))
                    for cc in range(2):
                        ci = c2 + cc
                        tg = None
                        for part in range(2):
                            wv, wk = wls[part]
                            c = part * 44 + ga * 4 + ci
                            if pre_hist:
                                b2 = next_mm()
                                for k in range(16):
                                    S.op("pe", lambda e, k=k, cc=cc, b2=b2, wv=wv: e.matmul(pmm[b2][:, 0:2], wv[:, k, cc * 128:(cc + 1) * 128],
                                                                                           xh2[:, k, :], start=(k == 0), stop=(k == 15)),
                                         reads=[wk, "xh2"], writes=[("pmm", b2)])
                                S.op("act", lambda e, c=c, b2=b2: e.copy(hist[:, c, :], pmm[b2][:, 0:2]),
                                     reads=[("pmm", b2)], writes=[("hist", c)])
                            b = next_mm()
                            for k in range(16):
                                S.op("pe", lambda e, k=k, cc=cc, b=b, wv=wv: e.matmul(pmm[b][:, 0:T], wv[:, k, cc * 128:(cc + 1) * 128],
                                                                                     xn[:, k, 0:T], start=(k == 0), stop=(k == 15)),
                                     reads=[wk, ("xn", k)], writes=[("pmm", b)])
                            th = next_tmp()
                            if conv_mode == "prompt":
                                S.op("act", lambda e, b=b, th=th: e.copy(tmpf[:, th, 2:2 + T], pmm[b][:, 0:T]),
                                     reads=[("pmm", b)], writes=[("tmpf", th)])
                                S.op("act", lambda e, c=c, th=th: e.copy(tmpf[:, th, 0:2], hist[:, c, :]),
                                     reads=[("hist", c)], writes=[("tmpf", th)])
                                S.op("act", lambda e, c=c, th=th: e.copy(hist[:, c, :], tmpf[:, th, T:T + 2]),
                                     reads=[("tmpf", th)], writes=[("hist", c)])
                                if not do_down:
                                    continue
                                ta = next_tmp()
                                hb = lambda o, th=th: tmpf[:, th, o:o + T]
                                av = lambda ta=ta: tmpf[:, ta, 0:T]
                            else:
                                ta = next_tmp()
                                hv = lambda th=th: tmpf[:, th, 0:72].rearrange("p (s i) -> p s i", i=18)
                                S.op("act", lambda e, b=b, hv=hv: e.copy(hv()[:, :, 2:18], pmm[b][:, 0:64].rearrange("p (s i) -> p s i", i=16)),
                                     reads=[("pmm", b)], writes=[("tmpf", th)])
                                S.op("act", lambda e, c=c, hv=hv: e.copy(hv()[:, :, 0:2], hs_hist[:, c, :, :]),
                                     reads=[("hs_hist", (c // 11) * 11)], writes=[("tmpf", th)])
                                S.op("act", lambda e, c=c, hv=hv: e.copy(hs_last[:, c, :, :], hv()[:, :, 16:18]),
                                     reads=[("tmpf", th)], writes=["hs_last"])
                                hb = lambda o, hv=hv: hv()[:, :, o:o + 16]
                                av = lambda ta=ta: tmpf[:, ta, 0:64].rearrange("p (s i) -> p s i", i=16)
                            S.op("dve", lambda e, c=c, hb=hb, av=av: e.tensor_scalar(av(), hb(2), convp[:, c, 2:3], convp[:, c, 3:4],
                                                                                   ALU.mult, ALU.add),
                                 reads=[("tmpf", th), "convp"], writes=[("tmpf", ta)])
                            S.op("dve", lambda e, c=c, hb=hb, av=av: e.scalar_tensor_tensor(av(), hb(1), convp[:, c, 1:2], av(),
                                                                                          ALU.mult, ALU.add),
                                 reads=[("tmpf", th), ("tmpf", ta), "convp"], writes=[("tmpf", ta)])
                            S.op("dve", lambda e, c=c, hb=hb, av=av: e.scalar_tensor_tensor(av(), hb(0), convp[:, c, 0:1], av(),
                                                                                          ALU.mult, ALU.add),
                                 reads=[("tmpf", th), ("tmpf", ta), "convp"], writes=[("tmpf", ta)])
                            if part == 0:
                                S.op("act", lambda e, ta=ta: e.activation(tmpf[:, ta, 0:T], tmpf[:, ta, 0:T], AF.Gelu),
                                     reads=[("tmpf", ta)], writes=[("tmpf", ta)])
                                tg = ta
                            else:
                                slot = (ga - ga0) * 4 + ci
                                S.op("dve", lambda e, tg=tg, ta=ta, slot=slot: e.tensor_tensor(act[:, slot, 0:T], tmpf[:, tg, 0:T],
                                                                                              tmpf[:, ta, 0:T], ALU.mult),
                                     reads=[("tmpf", tg), ("tmpf", ta)], writes=[("act", slot)])
            if not do_down:
                continue
            nch = (ga1 - ga0) * 4
            kc0 = ga0 * 4
            halves = [(0, nch // 2), (nch // 2, nch)]
            for f2 in range(0, 16, 2):
                banks = [next_mm(), next_mm()]
                for hi, (s0, s1) in enumerate(halves):
                    wv, wk = wload(w_down_s, "down", kc0 + s0, kc0 + s1, f2 * 128, (f2 + 2) * 128)
                    for cc in range(2):
                        for s in range(s0, s1):
                            S.op("pe", lambda e, s=s, s0=s0, cc=cc, wv=wv, bb=banks[cc]: e.matmul(
                                pmm[bb][:, 0:T], wv[:, s - s0, cc * 128:(cc + 1) * 128], act[:, s, 0:T],
                                start=(s == 0), stop=(s == nch - 1), skip_group_check=True),
                                reads=[wk, ("act", s)], writes=[("pmm", banks[cc])])
                if final_stats and ga1 == 11 and stt is None:
                    stt = Stats(T, lag=2)
                for cc in range(2):
                    f = f2 + cc
                    S.op("dve", lambda e, f=f, bb=banks[cc]: e.tensor_tensor(xT[:, f, 0:T], pmm[bb][:, 0:T], xT[:, f, 0:T], ALU.add),
                         reads=[("pmm", banks[cc]), ("x", f)], writes=[("x", f)])
                    if stt is not None:
                        stt.chunk(f)
            if stt is not None:
                stt.end()

    def final_out(T, out_rows, have_stats=True):
        if not have_stats:
            rms_stats(T, None)
        nbk = max(1, T // 128)
        ntok = min(T, 128)
        for kg in range(4):
            ts = [0, 1, 2, 3]
            for kk in range(4):
                k = kg * 4 + kk
                t = ts[kk]
                S.op("dve", lambda e, k=k, t=t: e.scalar_tensor_tensor(tmpf[:, t, 0:T], xT[:, k, 0:T], gcols[:, 32 + k:33 + k],
                                                                      rinv[:, 0:T], ALU.mult, ALU.mult),
                     reads=[("x", k), "gcols", "rinv"], writes=[("tmpf", t)])
            for j in range(nbk):
                b = next_mm()
                for kk in range(4):
                    S.op("pe", lambda e, kk=kk, b=b, j=j, t=ts[kk]: e.transpose(pmm[b][0:ntok, kk * 128:(kk + 1) * 128],
                                                                              tmpf[:, t, j * 128:j * 128 + ntok], ident[:]),
                         reads=[("tmpf", ts[kk]), "ident"], writes=[("pmm", b)])
                to = 4 + (st["fo"] % 2)
                st["fo"] += 1
                S.op("act", lambda e, b=b, to=to: e.copy(tmpf[0:ntok, to, 0:512], pmm[b][0:ntok, 0:512]),
                     reads=[("pmm", b)], writes=[("tmpf", to)])
                S.dma("pool", out_rows[j * 128:j * 128 + ntok, kg * 512:(kg + 1) * 512], tmpf[0:ntok, to, 0:512],
                      reads=[("tmpf", to)], writes=(), sem=("yst", to))

    def make_qkv_evacs(T, b0, out_kv):
        nb = T // 128
        slot0 = ring_slot(b0)

        def evq(c, b):
            S.op("act", lambda e, c=c, b=b: e.activation(act[:, c, 0:T], pmm[b][:, 0:T], AF.Copy, scale=float(128 ** -0.5)),
                 reads=[("pmm", b)], writes=[("act", c)])

        def evk(c, b):
            eng = ev_eng()
            keys = [("KT", c, slot0 + j) for j in range(nb)]
            if eng == "act":
                S.op("act", lambda e, c=c, b=b: e.copy(KT[:, c, slot0 * 128:slot0 * 128 + T], pmm[b][:, 0:T]),
                     reads=[("pmm", b)], writes=keys)
            else:
                S.op("dve", lambda e, c=c, b=b: e.tensor_copy(KT[:, c, slot0 * 128:slot0 * 128 + T], pmm[b][:, 0:T]),
                     reads=[("pmm", b)], writes=keys)

        def evv(j, g, b):
            sl = slot0 + j
            eng = "act" if out_kv else ev_eng()
            if eng == "act":
                S.op("act", lambda e, sl=sl, g=g, b=b: e.copy(VR[:, sl, g * 256:(g + 1) * 256], pmm[b][:, 0:256]),
                     reads=[("pmm", b)], writes=[("VR", sl, g)])
            else:
                S.op("dve", lambda e, sl=sl, g=g, b=b: e.tensor_copy(VR[:, sl, g * 256:(g + 1) * 256], pmm[b][:, 0:256]),
                     reads=[("pmm", b)], writes=[("VR", sl, g)])
            if out_kv:
                to = next_tmp()
                S.op("act", lambda e, b=b, to=to: e.copy(tmpf[:, to, 0:256], pmm[b][:, 0:256]),
                     reads=[("pmm", b)], writes=[("tmpf", to)])
                S.dma("act", nv_d[j * 128:(j + 1) * 128, g * 256:(g + 1) * 256], tmpf[:, to, 0:256],
                      reads=[("tmpf", to)], writes=(), sem=("tmpf", to))
        return evq, evk, evv

    def qkv_fillers(T, b0, out_kv):
        nb = T // 128
        evq, evk, evv = make_qkv_evacs(T, b0, out_kv)
        for pr in range(4):
            wq = wload(w_in_s, "in0", 0, 16, pr * 256, pr * 256 + 256)
            wkk = wload(w_in_s, "in0", 0, 16, 1024 + pr * 256, 1024 + pr * 256 + 256)
            for cc in range(2):
                for ((wv, wk), ev) in ((wq, evq), (wkk, evk)):
                    b = next_mm()
                    for k in range(16):
                        S.op("pe", lambda e, k=k, cc=cc, b=b, wv=wv: e.matmul(pmm[b][:, 0:T], wv[:, k, cc * 128:(cc + 1) * 128],
                                                                             xn[:, k, 0:T], start=(k == 0), stop=(k == 15)),
                             reads=[wk, ("xn", k)], writes=[("pmm", b)])
                        if k % 4 == 3 and k != 15:
                            yield None
                    ev(2 * pr + cc, b)
                    yield None
                if cc == 0:
                    wv, wk = wload(w_in_s, "in0", 0, 16, 2048 + pr * 256, 2048 + (pr + 1) * 256)
                    for j in range(nb):
                        b = next_mm()
                        for k in range(16):
                            S.op("pe", lambda e, k=k, b=b, wv=wv, j=j: e.matmul(pmm[b][0:128, 0:256], xn[:, k, j * 128:(j + 1) * 128],
                                                                               wv[:, k, :], start=(k == 0), stop=(k == 15)),
                                 reads=[wk, ("xn", k)], writes=[("pmm", b)])
                            if k % 4 == 3 and k != 15:
                                yield None
                        evv(j, pr, b)
                        yield None
                yield "head"

    def qkv_attention(T, b0, main_tile0, out_kv):
        gen = qkv_fillers(T, b0, out_kv)

        def drain_head():
            while True:
                r = next(gen, "end")
                if r in ("head", "end"):
                    return

        drain_head()
        for h in range(NH):
            state = {"done": h == NH - 1}
            npull = 3 if (h + 1) % 2 == 0 else 1

            def after_item():
                if state["done"]:
                    return
                for _ in range(npull):
                    r = next(gen, "end")
                    if r in ("head", "end"):
                        state["done"] = True
                        return
            attention_prompt(T, b0, main_tile0, heads=(h,), after_item=after_item)
            if not state["done"]:
                drain_head()

    def mixer_inputs_prompt(T, b0, kv_only, out_kv, main_tile0=False):
        nb = T // 128
        rms_stats(T, None)
        normalize_to_xn(T, 0)
        if kv_only:
            evq, evk, evv = make_qkv_evacs(T, b0, out_kv)
            proj_fm(T, w_in_s, "in0", 16, 1024, 8, lambda k: xn[:, k, 0:T], XN_ALL, evk)
            proj_tm(T, nb, 2048, evv, "in0")
            return

        vb_stage(T, [(j * 128, 128) for j in range(nb)], None)

        def evu(c, b):
            S.op("act", lambda e, c=c, b=b: e.activation(act[:, 8 + c, 0:T], pmm[b][:, 0:T], AF.Gelu),
                 reads=[("pmm", b)], writes=[("act", 8 + c)])
        proj_fm(T, w_in_s, "in1", 16, 3072, 8, lambda k: xn[:, k, 0:T], XN_ALL, evu)

        qkv_attention(T, b0, main_tile0, out_kv)
        if out_kv:
            def evko(j, g, b):
                to = next_tmp()
                S.op("act", lambda e, b=b, to=to: e.copy(tmpf[:, to, 0:256], pmm[b][:, 0:256]),
                     reads=[("pmm", b)], writes=[("tmpf", to)])
                S.dma("act", nk_d[j * 128:(j + 1) * 128, g * 256:(g + 1) * 256], tmpf[:, to, 0:256],
                      reads=[("tmpf", to)], writes=(), sem=("tmpf", to))
            proj_tm(T, nb, 1024, evko, "in0")

    def vb_stage(T, tok_slices, out_dram):
        tmps = {}

        def evvb(j, g, b):
            nt = tok_slices[j][1]
            if g == 0:
                tmps[j] = (next_tmp(), next_tmp())
            t = tmps[j][g // 2]
            o = (g % 2) * 256
            S.op("act", lambda e, b=b, t=t, o=o, nt=nt: e.activation(tmpf[0:nt, t, o:o + 256], pmm[b][0:nt, 0:256], AF.Gelu),
                 reads=[("pmm", b)], writes=[("tmpf", t)])
            if g % 2 == 1:
                tj = next_tmp()
                S.op("dve", lambda e, t=t, tj=tj, g=g, nt=nt: e.scalar_tensor_tensor(
                    tmpf[0:nt, tj, 0:512], tmpf[0:nt, t, 0:512], 1.0, tmpf[0:nt, t, 0:512], ALU.mult, ALU.mult,
                    accum_out=cols[0:nt, 1 + g // 2:2 + g // 2]),
                    reads=[("tmpf", t)], writes=[("tmpf", tj), ("cols", 1 + g // 2)])
            if g == 3:
                ta, tb = tmps[j]
                S.op("dve", lambda e, nt=nt: e.tensor_tensor(cols[0:nt, 3:4], cols[0:nt, 1:2], cols[0:nt, 2:3], ALU.add),
                     reads=[("cols", 1), ("cols", 2)], writes=[("cols", 3)])
                S.op("act", lambda e, nt=nt: e.activation(cols[0:nt, 4:5], cols[0:nt, 3:4], AF.Sqrt, bias=cols[0:nt, 0:1], scale=1.0 / 1024),
                     reads=[("cols", 3), "cols"], writes=[("cols", 4)])
                S.op("dve", lambda e, nt=nt: e.reciprocal(cols[0:nt, 5:6], cols[0:nt, 4:5]), reads=[("cols", 4)], writes=[("cols", 5)])
                for hh, t in enumerate((ta, tb)):
                    if out_dram is None:
                        dst = act[0:nt, 16 + 2 * j + hh, 0:512]
                        S.op("dve", lambda e, t=t, hh=hh, dst=dst, nt=nt: e.scalar_tensor_tensor(
                            dst, tmpf[0:nt, t, 0:512], cols[0:nt, 5:6], gsgu[0:nt, hh * 512:(hh + 1) * 512], ALU.mult, ALU.mult),
                            reads=[("tmpf", t), ("cols", 5), "gsgu"], writes=[("act", 16 + 2 * j + hh)])
                    else:
                        S.op("dve", lambda e, t=t, hh=hh, nt=nt: e.scalar_tensor_tensor(
                            tmpf[0:nt, t, 0:512], tmpf[0:nt, t, 0:512], cols[0:nt, 5:6], gsgu[0:nt, hh * 512:(hh + 1) * 512],
                            ALU.mult, ALU.mult),
                            reads=[("tmpf", t), ("cols", 5), "gsgu"], writes=[("tmpf", t)])
                        S.op("act", lambda e, t=t, hh=hh, nt=nt: e.copy(act[0:nt, 16 + hh, 0:512], tmpf[0:nt, t, 0:512]),
                             reads=[("tmpf", t)], writes=[("act", 16 + hh)])
                        S.dma("act", out_dram[0:nt, hh * 512:(hh + 1) * 512], tmpf[0:nt, t, 0:512],
                              reads=[("tmpf", t)], writes=(), sem=("tmpf", t))
        proj_tm(T, len(tok_slices), 4096, evvb, "in1", tok_slices)

    S.op("pool", lambda e: e.memset(cols[:, 0:1], EPS), writes=["cols"])
    xTr = xT_d

    def prompt_tile(col0, T, b0, kv_only=False, upto_h=False, main_idx=None):
        load_x(xTr[:, col0:col0 + T], T)
        last = (main_idx == 7)
        mixer_inputs_prompt(T, b0, kv_only, out_kv=last, main_tile0=(main_idx == 0))
        if kv_only:
            return
        sgu_prompt(T)
        merge(T)
        out_proj(T)
        normalize_to_xn(T, 16)
        if upto_h:
            S.op("act", lambda e: e.copy(xh2[:], xn[:, :, T - 2:T]), reads=XN_ALL, writes=["xh2"])
            return
        ffn(T, do_down=True, conv_mode="prompt", final_stats=True, pre_hist=(main_idx == 0 and do_halo))
        final_out(T, y_d[main_idx * 512:(main_idx + 1) * 512, :])

    if do_halo:
        prompt_tile(0, 512, -5, kv_only=True)
        prompt_tile(512, 128, -1, upto_h=True)
    for i in range(n_main):
        prompt_tile(HALO + 512 * i, 512, 4 * i, main_idx=i)
    if n_main == 8:
        for t2 in range(2):
            b = next_mm()
            S.op("pe", lambda e, b=b, t2=t2: e.transpose(pmm[b][0:88, 0:128], hist[:, :, t2], ident[:]),
                 reads=[("hist", c) for c in range(88)] + ["ident"], writes=[("pmm", b)])
            to = next_tmp()
            S.op("act", lambda e, b=b, to=to: e.copy(tmpf[0:88, to, 0:128], pmm[b][0:88, 0:128]),
                 reads=[("pmm", b)], writes=[("tmpf", to)])
            S.dma("act", ncp_d[t2].rearrange("(c p) -> c p", p=128), tmpf[0:88, to, 0:128],
                  reads=[("tmpf", to)], writes=(), sem=("tmpf", to))

    if do_sample:
        sample_tile(S, nc, locals())

    S.finish()
    return nc


def sample_tile(S, nc, L):
    g = L
    xT, xn, KT, VR, act, TB, tmpf, pbf = g["xT"], g["xn"], g["KT"], g["VR"], g["act"], g["TB"], g["tmpf"], g["pbf"]
    pmm, pS, pO, pD = g["pmm"], g["pS"], g["pO"], g["pD"]
    next_mm, next_tmp, next_pb, wload = g["next_mm"], g["next_tmp"], g["next_pb"], g["wload"]
    kTs, wbd, BS, onesb, hs_hist, hs_last, ident = g["kTs"], g["wbd"], g["BS"], g["onesb"], g["hs_hist"], g["hs_last"], g["ident"]
    w_in_s = g["w_in_s"]
    XN_ALL = g["XN_ALL"]
    T = 64
    for k in range(16):
        S.dma("sp", xT[:, k, 0:T], g["xsT_d"][k * 128:(k + 1) * 128, :], reads=(), writes=[("x", k)], sem=("x", k))
    cview = g["cconvT_d"].rearrange("(c p) (s t) -> p c s t", p=128, t=2)
    for c8 in range(0, 88, 11):
        S.dma("sp", hs_hist[:, c8:c8 + 11], cview[:, c8:c8 + 11], reads=(), writes=[("hs_hist", c8)], sem="hs_hist", total=True)
    g["rms_stats"](T, None)
    g["normalize_to_xn"](T, 0)

    def evq(c, b):
        S.op("act", lambda e, c=c, b=b: e.activation(act[:, c, 0:T], pmm[b][:, 0:T], AF.Copy, scale=float(128 ** -0.5)),
             reads=[("pmm", b)], writes=[("act", c)])
    g["proj_fm"](T, w_in_s, "in0", 16, 0, 8, lambda k: xn[:, k, 0:T], XN_ALL, evq)

    def evk(c, b):
        S.op("act", lambda e, c=c, b=b: e.copy(kTs[:, c, :], pmm[b][:, 0:T]), reads=[("pmm", b)], writes=["kTs"])
    g["proj_fm"](T, w_in_s, "in0", 16, 1024, 8, lambda k: xn[:, k, 0:T], XN_ALL, evk)

    def evko(j, gg, b):
        to = next_tmp()
        S.op("act", lambda e, b=b, to=to: e.copy(tmpf[0:64, to, 0:256], pmm[b][0:64, 0:256]), reads=[("pmm", b)], writes=[("tmpf", to)])
        S.dma("act", g["nks_d"][:, gg * 256:(gg + 1) * 256], tmpf[0:64, to, 0:256], reads=[("tmpf", to)], writes=(), sem=("tmpf", to))
    g["proj_tm"](T, 1, 1024, evko, "in0", [(0, 64)])

    def evv(j, gg, b):
        s = j
        S.op("act", lambda e, s=s, gg=gg, b=b: e.copy(VR[0:16, 4 + s, gg * 256:(gg + 1) * 256], pmm[b][0:16, 0:256]),
             reads=[("pmm", b)], writes=[("VR", 4 + s, gg)])
        to = next_tmp()
        S.op("act", lambda e, b=b, to=to: e.copy(tmpf[0:16, to, 0:256], pmm[b][0:16, 0:256]), reads=[("pmm", b)], writes=[("tmpf", to)])
        S.dma("act", g["nvs_d"][16 * s:16 * s + 16, gg * 256:(gg + 1) * 256], tmpf[0:16, to, 0:256],
              reads=[("tmpf", to)], writes=(), sem=("tmpf", to))
    g["proj_tm"](T, 4, 2048, evv, "in0", [(16 * s, 16) for s in range(4)])

    def evu(c, b):
        S.op("act", lambda e, c=c, b=b: e.activation(act[:, 8 + c, 0:T], pmm[b][:, 0:T], AF.Gelu), reads=[("pmm", b)], writes=[("act", 8 + c)])
    g["proj_fm"](T, w_in_s, "in1", 16, 3072, 8, lambda k: xn[:, k, 0:T], XN_ALL, evu)
    g["vb_stage"](T, [(0, 64)], g["nvbs_d"])

    first = True
    for s in range(4):
        half = (s % 2) * 512
        kkeys = [("KT", h, 4 * (s % 2) + j) for h in range(8) for j in range(4)]
        S.dma("pool", KT[:, :, half:half + 512], g["ckT_d"][s].rearrange("h d k -> d h k"), reads=(), writes=kkeys,
              sem=("KTc", s % 2))
        vkeys = [("VR", j, gq) for j in range(4) for gq in range(4)]
        S.dma("pool", VR[:, 0:4, :], g["cv_d"][s].rearrange("(b p) f -> p b f", p=128), reads=(), writes=vkeys, sem="VRc")
        for h in range(8):
            for kt in range(4):
                cb = (h * 5 + kt) * 16
                S.op("pe", lambda e, h=h, kt=kt, cb=cb, s=s, half=half: e.matmul(
                    pS[:, cb:cb + 16], KT[:, h, half + kt * 128:half + (kt + 1) * 128], act[:, h, 16 * s:16 * s + 16],
                    start=True, stop=True),
                    reads=[("KT", h, 4 * (s % 2) + kt), ("act", h)], writes=[("pS", 0), ("pS", 512)])
            cb = (h * 5 + 4) * 16
            S.op("pe", lambda e, h=h, cb=cb, s=s: e.matmul(
                pS[0:16, cb:cb + 16], kTs[:, h, 16 * s:16 * s + 16], act[:, h, 16 * s:16 * s + 16], start=True, stop=True),
                reads=["kTs", ("act", h)], writes=[("pS", 0), ("pS", 512)])
        t0, t1 = next_tmp(), next_tmp()

        def sview(c):
            return (t0, c) if c < 512 else (t1, c - 512)
        for h in range(8):
            for kt in range(5):
                cb = (h * 5 + kt) * 16
                m0 = 512 - 128 * kt if kt < 4 else 0
                ti, off = sview(cb)
                np_ = 128 if kt < 4 else 16
                S.op("dve", lambda e, h=h, cb=cb, m0=m0, ti=ti, off=off, np_=np_: e.scalar_tensor_tensor(
                    tmpf[0:np_, ti, off:off + 16], pS[0:np_, cb:cb + 16], 60.0, TB[0:np_, h, m0:m0 + 16], ALU.min, ALU.add),
                    reads=[("pS", 0), ("pS", 512), "TB"], writes=[("tmpf", ti)])
        p = next_pb()
        S.op("act", lambda e, p=p, t0=t0: e.activation(pbf[:, p, 0:512], tmpf[:, t0, 0:512], AF.Exp),
             reads=[("tmpf", t0)], writes=[("pb", p)])
        S.op("act", lambda e, p=p, t1=t1: e.activation(pbf[:, p, 512:640], tmpf[:, t1, 0:128], AF.Exp),
             reads=[("tmpf", t1)], writes=[("pb", p)])
        for h in range(8):
            ob = (s * 8 + h) * 16
            for kt in range(5):
                cb = (h * 5 + kt) * 16
                if kt < 4:
                    lv = VR[:, kt, h * 128:(h + 1) * 128]
                    lo = onesb[:]
                    rp = pbf[:, p, cb:cb + 16]
                    rk = [("VR", kt, h // 2)]
                else:
                    lv = VR[0:16, 4 + s, h * 128:(h + 1) * 128]
                    lo = onesb[0:16, :]
                    rp = pbf[0:16, p, cb:cb + 16]
                    rk = [("VR", 4 + s, h // 2)]
                S.op("pe", lambda e, lv=lv, rp=rp, ob=ob, first=first: e.matmul(pO[:, ob:ob + 16], lv, rp, start=first, stop=False,
                                                                               skip_group_check=True),
                     reads=rk + [("pb", p)], writes=["pO"])
                S.op("pe", lambda e, lo=lo, rp=rp, ob=ob, first=first: e.matmul(pD[:, ob:ob + 16], lo, rp, start=first, stop=False,
                                                                               skip_group_check=True),
                     reads=["onesb", ("pb", p)], writes=["pD"])
                first = False
    t = next_tmp()
    S.op("dve", lambda e, t=t: e.reciprocal(tmpf[:, t, 0:512], pD[:, 0:512]), reads=["pD"], writes=[("tmpf", t)])
    S.op("dve", lambda e, t=t: e.tensor_tensor(tmpf[:, t, 0:512], pO[:, 0:512], tmpf[:, t, 0:512], ALU.mult),
         reads=["pO", ("tmpf", t)], writes=[("tmpf", t)])
    for h in range(8):
        S.op("act", lambda e, t=t, h=h: e.copy(act[:, 24 + h, 0:64].rearrange("p (s i) -> p s i", i=16),
                                              tmpf[:, t, 0:512].rearrange("p (s h i) -> p h s i", s=4, h=8)[:, h, :, :]),
             reads=[("tmpf", t)], writes=[("act", 24 + h)])

    for gg in range(8):
        b = next_mm()
        S.op("pe", lambda e, gg=gg, b=b: e.matmul(pmm[b][:, 0:64], act[0:64, 16 + gg // 4, (gg % 4) * 128:(gg % 4 + 1) * 128],
                                                  wbd[:, gg, :], start=True, stop=True),
             reads=[("act", 16 + gg // 4)] + [("wbdl", q) for q in range(4)], writes=[("pmm", b)])
        t = next_tmp()
        S.op("dve", lambda e, gg=gg, b=b, t=t: e.tensor_tensor(
            tmpf[:, t, 0:64].rearrange("p (s i) -> p s i", i=16), pmm[b][:, 0:64].rearrange("p (s i) -> p s i", i=16),
            BS[:, gg, 0:16].unsqueeze(1).to_broadcast([128, 4, 16]), ALU.add),
            reads=[("pmm", b), "BS"], writes=[("tmpf", t)])
        S.op("dve", lambda e, gg=gg, t=t: e.tensor_tensor(act[:, gg, 0:64], tmpf[:, t, 0:64], act[:, 8 + gg, 0:64], ALU.mult),
             reads=[("tmpf", t), ("act", 8 + gg)], writes=[("act", gg)])

    g["merge"](T)
    g["out_proj"](T)
    g["normalize_to_xn"](T, 16)
    g["ffn"](T, do_down=True, conv_mode="sample", final_stats=True)
    g["final_out"](T, g["ys_d"])
    for s in range(4):
        for t2 in range(2):
            b = next_mm()
            S.op("pe", lambda e, b=b, s=s, t2=t2: e.transpose(pmm[b][0:88, 0:128], hs_last[:, :, s, t2], ident[:]),
                 reads=["hs_last", "ident"], writes=[("pmm", b)])
            to = next_tmp()
            S.op("act", lambda e, b=b, to=to: e.copy(tmpf[0:88, to, 0:128], pmm[b][0:88, 0:128]), reads=[("pmm", b)], writes=[("tmpf", to)])
            S.dma("act", g["ncs_d"][s, t2].rearrange("(c p) -> c p", p=128), tmpf[0:88, to, 0:128],
                  reads=[("tmpf", to)], writes=(), sem=("tmpf", to))


_NC_CACHE = {}


def _consts(rel_bias):
    r = np.arange(128)[:, None]
    m = np.arange(640)[None, :]
    idx = np.clip(m - r, -256, 256) + 256
    tbg = np.ascontiguousarray(rel_bias[:, idx]).astype(np.float32)
    inval = ((r < 64) & (m >= 576)) | ((r >= 64) & (m < 64))
    mk = np.where(inval, np.float32(NEG), np.float32(0.0)).astype(np.float32)
    triu = np.triu(np.ones((128, 128), np.float32))
    bd = np.zeros((64, 64), np.float32)
    for s in range(4):
        bd[16 * s:16 * s + 16, 16 * s:16 * s + 16] = np.triu(np.ones((16, 16), np.float32))
    return tbg, mk, triu, bd


def make_in_maps(inp, n_cores=8):
    x_prompt = np.asarray(inp["x_prompt"], np.float32)
    x_sample = np.asarray(inp["x_sample"], np.float32)
    cache_k = np.asarray(inp["cache_k"], np.float32)[0]
    cache_v = np.asarray(inp["cache_v"], np.float32)[0]
    cconv = np.asarray(inp["cache_ffn_conv"], np.float32)[0]
    tbg, mk, triu, bd = _consts(np.asarray(inp["rel_bias"], np.float32)[0])
    gcols = np.concatenate([np.asarray(inp[n], np.float32).reshape(16, 128).T
                            for n in ("norm_mix_g", "norm_ffn_g", "norm_final_g")], axis=1)
    conv_w = np.asarray(inp["conv_w"], np.float32)[0]
    conv_b = np.asarray(inp["conv_b"], np.float32)[0]
    cp = np.stack([conv_w[0], conv_w[1], conv_w[2], conv_b], axis=-1)
    convp = np.ascontiguousarray(cp.reshape(88, 128, 4).transpose(1, 0, 2)).reshape(128, 88 * 4)
    shared = {
        "w_in": np.ascontiguousarray(np.asarray(inp["w_in"], np.float32)[0]),
        "w_a": np.ascontiguousarray(np.asarray(inp["w_branch_a"], np.float32)[0]),
        "w_b": np.ascontiguousarray(np.asarray(inp["w_branch_b"], np.float32)[0]),
        "w_out": np.ascontiguousarray(np.asarray(inp["w_out"], np.float32)[0]),
        "w_up": np.ascontiguousarray(np.asarray(inp["w_up"], np.float32)[0]),
        "w_down": np.ascontiguousarray(np.asarray(inp["w_down"], np.float32)[0]),
        "gcols": np.ascontiguousarray(gcols),
        "gsgu": np.ascontiguousarray(np.asarray(inp["sgu_norm_g"], np.float32)[0]),
        "convp": convp,
        "wsT": np.ascontiguousarray(np.asarray(inp["w_s"], np.float32)[0].transpose(0, 2, 1)),
        "bs": np.ascontiguousarray(np.asarray(inp["b_s"], np.float32)[0].reshape(1024)),
        "tbg": tbg, "mk": mk, "ident": np.eye(128, dtype=np.float32), "triu": triu, "maskbd": bd,
    }
    maps = []
    for c in range(n_cores):
        b, half = c // 2, c % 2
        xT = np.zeros((D, HALO + NTOK), np.float32)
        if half == 0:
            xT[:, HALO:] = x_prompt[b, 0:NTOK].T
        else:
            xT[:, :] = x_prompt[b, NTOK - HALO:2 * NTOK].T
        m = dict(shared)
        m["xT"] = xT
        m["hones"] = np.full((128, 128), float(half), np.float32)
        m["xsT"] = np.ascontiguousarray(x_sample[4 * c:4 * c + 4].reshape(64, D).T)
        m["ckT"] = np.ascontiguousarray(cache_k[4 * c:4 * c + 4].transpose(0, 2, 3, 1))
        m["cv"] = np.ascontiguousarray(cache_v[4 * c:4 * c + 4].reshape(4, 512, 1024))
        m["cconvT"] = np.ascontiguousarray(cconv[4 * c:4 * c + 4].transpose(2, 0, 1).reshape(2 * DFF, 8))
        maps.append(m)
    return maps


def assemble(results):
    y_prompt = np.zeros((4, 8192, D), np.float32)
    y_sample = np.zeros((32, 16, D), np.float32)
    nkp = np.zeros((1, 4, 512, 8, 128), np.float32)
    nvp = np.zeros((1, 4, 512, 8, 128), np.float32)
    nks = np.zeros((1, 32, 16, 8, 128), np.float32)
    nvs = np.zeros((1, 32, 16, 8, 128), np.float32)
    nvbs = np.zeros((1, 32, 16, 1024), np.float32)
    ncp = np.zeros((1, 4, 2, 2 * DFF), np.float32)
    ncs = np.zeros((1, 32, 2, 2 * DFF), np.float32)
    for c, r in enumerate(results):
        b, half = c // 2, c % 2
        y_prompt[b, half * NTOK:(half + 1) * NTOK] = r["y"]
        y_sample[4 * c:4 * c + 4] = r["ys"].reshape(4, 16, D)
        if half == 1:
            nkp[0, b] = r["nk"].reshape(512, 8, 128)
            nvp[0, b] = r["nv"].reshape(512, 8, 128)
            ncp[0, b] = r["ncp"]
        nks[0, 4 * c:4 * c + 4] = r["nks"].reshape(4, 16, 8, 128)
        nvs[0, 4 * c:4 * c + 4] = r["nvs"].reshape(4, 16, 8, 128)
        nvbs[0, 4 * c:4 * c + 4] = r["nvbs"].reshape(4, 16, 1024)
        ncs[0, 4 * c:4 * c + 4] = r["ncs"]
    return (y_prompt, y_sample, nkp, nvp, nks, nvs, nvbs, ncp, ncs)


def kernel(**inputs):
    if "nc" not in _NC_CACHE:
        _NC_CACHE["nc"] = build()
    nc = _NC_CACHE["nc"]
    maps = make_in_maps(inputs)
    res = run_bass_kernel_spmd(nc, maps, core_ids=list(range(8)))
    return assemble(res.results)
```

```python
import numpy as np
import concourse.bass as bass
import concourse.mybir as mybir
from concourse.bass_utils import run_bass_kernel_spmd

F32 = mybir.dt.float32
BF16 = mybir.dt.bfloat16
AF = mybir.ActivationFunctionType
ALU = mybir.AluOpType

D = 2048
DIN = 9216
DFF = 5632
NH = 8
HALO = 640
NTOK = 4096
EPS = 1e-6
NEG = -30000.0


class _Op:
    __slots__ = ("eng", "fn", "idx", "deps", "signal", "sig_val", "dma_key", "dma_val", "waits", "gid", "total")


class Sched:
    ENGS = ("pe", "act", "dve", "pool", "sp")

    def __init__(self, nc):
        self.nc = nc
        self.ops = {e: [] for e in self.ENGS}
        self.lastw = {}
        self.readers = {}
        self.dma_cnt = {}
        self.nops = 0

    def _add(self, eng, fn, reads, writes):
        op = _Op()
        op.eng = eng
        op.fn = fn
        op.signal = False
        op.sig_val = 0
        op.dma_key = None
        op.dma_val = 0
        op.total = False
        op.gid = self.nops
        self.nops += 1
        deps = {}
        lw = self.lastw
        rd = self.readers
        for k in reads:
            w = lw.get(k)
            if w is not None:
                deps[w.gid] = w
        for k in writes:
            w = lw.get(k)
            if w is not None:
                deps[w.gid] = w
            for r in rd.get(k, ()):
                deps[r.gid] = r
        inorder = eng in ("pe", "act", "dve", "pool")
        for k in reads:
            lst = rd.setdefault(k, [])
            if inorder and not getattr(self, "_dma_building", False):
                for i_, r in enumerate(lst):
                    if r.eng == eng and r.dma_key is None:
                        lst[i_] = op
                        break
                else:
                    lst.append(op)
            else:
                lst.append(op)
        for k in writes:
            lw[k] = op
            rd[k] = []
        deps.pop(op.gid, None)
        op.deps = list(deps.values())
        op.idx = len(self.ops[eng])
        self.ops[eng].append(op)
        return op

    def op(self, eng, fn, reads=(), writes=()):
        return self._add(eng, fn, reads, writes)

    def dma(self, eng, out, in_, reads=(), writes=(), sem=None, total=False):
        self._dma_building = True
        op = self._add(eng, lambda e, out=out, in_=in_: e.dma_start(out=out, in_=in_), reads, writes)
        self._dma_building = False
        op.dma_key = sem
        op.total = total
        c = self.dma_cnt.get(sem, 0) + 1
        self.dma_cnt[sem] = c
        op.dma_val = 16 * c
        return op

    def finish(self, same_eng_dist=4):
        nc = self.nc
        for e in self.ENGS:
            for op in self.ops[e]:
                need = []
                for d in op.deps:
                    if d.dma_key is not None:
                        need.append(d)
                    elif d.eng == op.eng:
                        if op.dma_key is not None:
                            d.signal = True
                            need.append(d)
                        elif e == "pe":
                            continue
                        elif op.idx - d.idx <= same_eng_dist:
                            d.signal = True
                            need.append(d)
                    else:
                        d.signal = True
                        need.append(d)
                op.deps = need
        eng_sem = {}
        for e in self.ENGS:
            cnt = 0
            for op in self.ops[e]:
                if op.signal:
                    cnt += 1
                    op.sig_val = cnt
                if op.dma_key is not None and op.total:
                    op.dma_val = 16 * self.dma_cnt[op.dma_key]
            eng_sem[e] = nc.alloc_semaphore(f"s_{e}")
            print(f"[sched] {e}: {len(self.ops[e])} ops, {cnt} signals", flush=True)
        dma_sem = {k: nc.alloc_semaphore(f"d_{i}") for i, k in enumerate(self.dma_cnt)}
        print(f"[sched] {len(dma_sem)} dma sems", flush=True)
        for e in self.ENGS:
            waited = {}
            for op in self.ops[e]:
                req = {}
                for d in op.deps:
                    if d.dma_key is not None:
                        key = ("d", d.dma_key)
                        sem = dma_sem[d.dma_key]
                        val = d.dma_val
                    else:
                        key = ("e", d.eng)
                        sem = eng_sem[d.eng]
                        val = d.sig_val
                    if val > req.get(key, (None, 0))[1]:
                        req[key] = (sem, val)
                w = []
                for key, (sem, val) in req.items():
                    if val > waited.get(key, 0):
                        waited[key] = val
                        w.append((sem, val))
                op.waits = w
        engobj = {"pe": "tensor", "act": "scalar", "dve": "vector", "pool": "gpsimd", "sp": "sync"}
        final_waits = [(dma_sem[k], 16 * c) for k, c in self.dma_cnt.items()]

        def run(e, eng):
            es = eng_sem[e]
            for op in self.ops[e]:
                for sem, val in op.waits:
                    eng.wait_ge(sem, val)
                ins = op.fn(eng)
                if op.signal:
                    ins.then_inc(es, 1)
                if op.dma_key is not None:
                    ins.then_inc(dma_sem[op.dma_key], 16)
            if e == "sp":
                for sem, val in final_waits:
                    eng.wait_ge(sem, val)

        with nc.Block() as block:
            for e in self.ENGS:
                if not self.ops[e] and e != "sp":
                    continue
                getattr(block, engobj[e])(lambda eng, e=e: run(e, eng))


def build(n_main=8, do_halo=True, do_sample=True):
    nc = bass.Bass("TRN2", target_bir_lowering=False)
    S = Sched(nc)

    def din(name, shape, dt=F32):
        return nc.dram_tensor(name, list(shape), dt, kind="ExternalInput").ap()

    def dout(name, shape):
        return nc.dram_tensor(name, list(shape), F32, kind="ExternalOutput").ap()

    def dscr(name, shape):
        return nc.dram_tensor(name, list(shape), BF16, kind="Internal").ap()

    xT_d = din("xT", [D, HALO + NTOK])
    xsT_d = din("xsT", [D, 64])
    ckT_d = din("ckT", [4, NH, 128, 512])
    cv_d = din("cv", [4, 512, 1024])
    cconvT_d = din("cconvT", [2 * DFF, 8])
    w_in_d = din("w_in", [D, DIN])
    w_a_d = din("w_a", [1024, D])
    w_b_d = din("w_b", [1024, D])
    w_out_d = din("w_out", [D, D])
    w_up_d = din("w_up", [D, 2 * DFF])
    w_down_d = din("w_down", [DFF, D])
    gcols_d = din("gcols", [128, 48])
    gsgu_d = din("gsgu", [1024])
    convp_d = din("convp", [128, 88 * 4])
    wsT_d = din("wsT", [NH, 128, 128])
    bs_d = din("bs", [1024])
    tbg_d = din("tbg", [NH, 128, 640])
    mk_d = din("mk", [128, 640])
    ident_d = din("ident", [128, 128])
    hones_d = din("hones", [128, 128])
    triu_d = din("triu", [128, 128])
    maskbd_d = din("maskbd", [64, 64])

    w_in_s = dscr("w_in_s", [D, DIN])
    w_a_s = dscr("w_a_s", [1024, D])
    w_b_s = dscr("w_b_s", [1024, D])
    w_out_s = dscr("w_out_s", [D, D])
    w_up_s = dscr("w_up_s", [D, 2 * DFF])
    w_down_s = dscr("w_down_s", [DFF, D])

    NBLK = 144
    wscr = dscr("wscr", [NBLK, 128, 4096])

    y_d = dout("y", [NTOK, D])
    ys_d = dout("ys", [64, D])
    nk_d = dout("nk", [512, 1024])
    nv_d = dout("nv", [512, 1024])
    nks_d = dout("nks", [64, 1024])
    nvs_d = dout("nvs", [64, 1024])
    nvbs_d = dout("nvbs", [64, 1024])
    ncp_d = dout("ncp", [2, 2 * DFF])
    ncs_d = dout("ncs", [4, 2, 2 * DFF])

    sb = nc.alloc_sbuf_tensor
    xT = sb("xT_sb", [128, 16, 512], F32)
    xn = sb("xn_sb", [128, 16, 512], BF16)
    KT = sb("KT_sb", [128, NH, 1024], BF16)
    VR = sb("VR_sb", [128, 8, 1024], BF16)
    act = sb("act_sb", [128, 32, 512], BF16)
    TB = sb("TB_sb", [128, NH, 640], F32)
    NTMP = 6
    tmpf = sb("tmpf_sb", [128, NTMP, 514], F32)
    NPB = 3
    pbf = sb("pbf_sb", [128, NPB, 640], BF16)
    NWB = 4
    wbuf = sb("wbuf_sb", [128, NWB, 4096], BF16)
    rinv = sb("rinv_sb", [128, 512], F32)
    gcols = sb("gcols_sb", [128, 48], F32)
    gsgu = sb("gsgu_sb", [128, 1024], F32)
    convp = sb("convp_sb", [128, 88, 4], F32)
    wsT = sb("wsT_sb", [128, NH, 128], BF16)
    wbd = sb("wbd_sb", [64, NH, 64], BF16)
    BS = sb("BS_sb", [128, NH, 128], F32)
    ident = sb("ident_sb", [128, 128], F32)
    onesb = sb("onesb_sb", [128, 128], BF16)
    honesb = sb("honesb_sb", [128, 128], BF16)
    onesD = sb("onesD_sb", [128, 128], BF16)
    hist = sb("hist_sb", [128, 88, 2], F32)
    hs_hist = sb("hs_hist_sb", [128, 88, 4, 2], F32)
    hs_last = sb("hs_last_sb", [128, 88, 4, 2], F32)
    kTs = sb("kTs_sb", [128, NH, 64], BF16)
    cols = sb("cols_sb", [128, 8], F32)
    xh2 = sb("xh2_sb", [128, 16, 2], BF16)
    print("[build] sbuf bytes remaining", nc.sbuf_bytes_remaining, flush=True)

    pmm = [nc.alloc_psum_tensor(f"pmm{i}", [128, 512], F32) for i in range(4)]
    pS = nc.alloc_psum_tensor("pS", [128, 1024], F32)
    pO = nc.alloc_psum_tensor("pO", [128, 512], F32)
    pD = nc.alloc_psum_tensor("pD", [128, 512], F32)

    st = {"mm": 0, "tmp": 0, "pb": 0, "wb": 0, "ev": 0, "fo": 0}

    resv = set()

    def next_mm():
        while True:
            i = st["mm"] % 4
            st["mm"] += 1
            if i not in resv:
                return i

    def next_tmp():
        i = st["tmp"] % NTMP
        st["tmp"] += 1
        return i

    def next_pb():
        i = st["pb"] % NPB
        st["pb"] += 1
        return i

    def ev_eng():
        st["ev"] += 1
        return "act" if st["ev"] % 2 else "dve"

    def setup_load(dst, src, key):
        S.dma("sp", dst, src, reads=(), writes=[key], sem="setup", total=True)

    setup_load(gcols[:], gcols_d, "gcols")
    setup_load(gsgu[:], gsgu_d.partition_broadcast(128), "gsgu")
    setup_load(convp[:], convp_d.rearrange("p (c f) -> p c f", f=4), "convp")
    setup_load(BS[:], bs_d.partition_broadcast(128).rearrange("p (g i) -> p g i", i=128), "BS")
    setup_load(ident[:], ident_d, "ident")
    setup_load(TB[:], tbg_d.rearrange("h r m -> r h m"), "TB")
    S.op("pool", lambda e: e.memset(onesb[:], 1.0), writes=["onesb"])
    S.op("pool", lambda e: e.memset(onesD[:], 1.0 / D), writes=["onesD"])
    S.op("pool", lambda e: e.memset(hist[:], 0.0), writes=[("hist", c) for c in range(88)])
    S.op("pool", lambda e: e.memset(wbd[:], 0.0), writes=["wbd0"])
    setup_load(tmpf[:, 0, 0:128], hones_d, ("tmpf", 0))
    S.op("act", lambda e: e.copy(honesb[:], tmpf[:, 0, 0:128]), reads=[("tmpf", 0)], writes=["honesb"])
    setup_load(tmpf[:, 1, 0:512], mk_d[:, 0:512], ("tmpf", 1))
    setup_load(tmpf[:, 2, 0:128], mk_d[:, 512:640], ("tmpf", 2))
    for h in range(NH):
        S.op("dve", lambda e, h=h: e.tensor_tensor(TB[:, h, 0:512], TB[:, h, 0:512], tmpf[:, 1, 0:512], ALU.add),
             reads=["TB", ("tmpf", 1)], writes=["TB"])
        S.op("dve", lambda e, h=h: e.tensor_tensor(TB[:, h, 512:640], TB[:, h, 512:640], tmpf[:, 2, 0:128], ALU.add),
             reads=["TB", ("tmpf", 2)], writes=["TB"])
    setup_load(tmpf[:, 3, 0:128], triu_d, ("tmpf", 3))
    for g in range(NH):
        t = 4 + (g % 2)
        S.dma("sp", tmpf[:, t, 0:128], wsT_d[g], reads=(), writes=[("tmpf", t)], sem=("tmpf", t))
        S.op("dve", lambda e, g=g, t=t: e.tensor_tensor(wsT[:, g, :], tmpf[:, t, 0:128], tmpf[:, 3, 0:128], ALU.mult),
             reads=[("tmpf", t), ("tmpf", 3)], writes=["wsT"])
    if do_sample:
        for s_ in range(4):
            S.dma("sp", wbd[16 * s_:16 * s_ + 16, :, 16 * s_:16 * s_ + 16], wsT[0:16, :, 0:16],
                  reads=["wsT", "wbd0"], writes=[("wbdl", s_)], sem="wbdl", total=True)

    wtag = {id(w_in_s): ("in", w_in_d), id(w_a_s): ("a", w_a_d), id(w_b_s): ("b", w_b_d), id(w_out_s): ("out", w_out_d),
            id(w_up_s): ("up", w_up_d), id(w_down_s): ("down", w_down_d)}
    scr_done = set()
    scr_idx = {}
    scr_touch = {}
    scr_wt = {}
    WB_SPREAD = 1

    def wload(scr, castkey, k0, k1, c0, c1):
        i = st["wb"] % NWB
        st["wb"] += 1
        nk = k1 - k0
        ncol = c1 - c0
        assert nk * ncol <= 4096
        view = wbuf[:, i, 0:nk * ncol].rearrange("p (k c) -> p k c", c=ncol)
        flat = wbuf[:, i, 0:nk * ncol]
        tag, w32 = wtag[id(scr)]
        rk = ("scr", tag, k0, k1, c0, c1)
        if rk in scr_done:
            S.dma("sp", flat, wscr[scr_idx[rk], :, 0:nk * ncol], reads=[rk], writes=[("wb", i)], sem=("wb", i))
        else:
            scr_idx[rk] = len(scr_idx)
            assert scr_idx[rk] < NBLK
            src32 = w32.rearrange("(k p) c -> p k c", p=128)[:, k0:k1, c0:c1]
            S.dma("pool", view, src32, reads=(), writes=[("wb", i)], sem=("wbc", i))
            scr_done.add(rk)
            S.dma("sp", wscr[scr_idx[rk], :, 0:nk * ncol], flat, reads=[("wb", i)], writes=[rk], sem=("wbst", i))
        return view, ("wb", i)

    def in_castkey(c0):
        return "in0" if c0 < 3072 else ("in1" if c0 < 5120 else "in2")

    class Stats:
        def __init__(self, T, lag=2, sq_eng="act"):
            self.sq_eng = sq_eng
            self.T = T
            self.b = next_mm()
            resv.add(self.b)
            self.n = 0
            self.pend = []
            self.lag = lag

        def chunk(self, k):
            T, b = self.T, self.b
            p = next_pb()
            if self.sq_eng == "act":
                S.op("act", lambda e, k=k, p=p: e.activation(pbf[:, p, 0:T], xT[:, k, 0:T], AF.Square),
                     reads=[("x", k)], writes=[("pb", p)])
            else:
                S.op("dve", lambda e, k=k, p=p: e.tensor_tensor(pbf[:, p, 0:T], xT[:, k, 0:T], xT[:, k, 0:T], ALU.mult),
                     reads=[("x", k)], writes=[("pb", p)])
            self.pend.append(p)
            while len(self.pend) > self.lag:
                self._mm()

        def _mm(self):
            T, b = self.T, self.b
            p = self.pend.pop(0)
            first = (self.n == 0)
            last = (self.n == 15)
            self.n += 1
            S.op("pe", lambda e, p=p, b=b, first=first, last=last: e.matmul(pmm[b][:, 0:T], onesD[:], pbf[:, p, 0:T],
                                                                           start=first, stop=last, skip_group_check=True),
                 reads=["onesD", ("pb", p)], writes=[("pmm", b)])

        def end(self):
            T, b = self.T, self.b
            while self.pend:
                self._mm()
            assert self.n == 16
            S.op("act", lambda e, b=b: e.activation(rinv[:, 0:T], pmm[b][:, 0:T], AF.Ln, bias=cols[:, 0:1], scale=1.0),
                 reads=[("pmm", b), "cols"], writes=["rinv"])
            S.op("act", lambda e: e.activation(rinv[:, 0:T], rinv[:, 0:T], AF.Exp, scale=-0.5), reads=["rinv"], writes=["rinv"])
            resv.discard(b)

    def rms_stats(T, src_keys):
        stt = Stats(T, lag=2 if NPB >= 3 else 1, sq_eng="dve")
        for k in range(16):
            stt.chunk(k)
        stt.end()

    def normalize_to_xn(T, gbase):
        for k in range(16):
            S.op("dve", lambda e, k=k: e.scalar_tensor_tensor(xn[:, k, 0:T], xT[:, k, 0:T], gcols[:, gbase + k:gbase + k + 1],
                                                              rinv[:, 0:T], ALU.mult, ALU.mult),
                 reads=[("x", k), "gcols", "rinv"], writes=[("xn", k)])

    XN_ALL = [("xn", k) for k in range(16)]

    def proj_fm(T, scr, castkey, kchunks, col0, nchunks, rhs_of, rhs_keys, evac):
        for c2 in range(0, nchunks, 2):
            ncc = min(2, nchunks - c2)
            wv, wk = wload(scr, castkey, 0, kchunks, col0 + c2 * 128, col0 + (c2 + ncc) * 128)
            for cc in range(ncc):
                b = next_mm()
                for k in range(kchunks):
                    S.op("pe", lambda e, k=k, cc=cc, b=b, wv=wv: e.matmul(pmm[b][:, 0:T], wv[:, k, cc * 128:(cc + 1) * 128],
                                                                         rhs_of(k), start=(k == 0), stop=(k == kchunks - 1)),
                         reads=[wk, rhs_keys[k]], writes=[("pmm", b)])
                evac(c2 + cc, b)

    def proj_tm(T, nb, col0, evac, castkey, tok_slices=None):
        if tok_slices is None:
            tok_slices = [(j * 128, 128) for j in range(nb)]
        wl = [wload(w_in_s, castkey, 0, 16, col0 + g * 256, col0 + (g + 1) * 256) for g in range(4)]
        for j, (t0, nt) in enumerate(tok_slices):
            for g in range(4):
                wv, wk = wl[g]
                b = next_mm()
                for k in range(16):
                    S.op("pe", lambda e, k=k, b=b, wv=wv, t0=t0, nt=nt: e.matmul(pmm[b][0:nt, 0:256], xn[:, k, t0:t0 + nt],
                                                                                wv[:, k, :], start=(k == 0), stop=(k == 15)),
                         reads=[wk, ("xn", k)], writes=[("pmm", b)])
                evac(j, g, b)

    def load_x(src_ap_cols, T):
        for k in range(16):
            S.dma("sp", xT[:, k, 0:T], src_ap_cols[k * 128:(k + 1) * 128, :], reads=(), writes=[("x", k)], sem=("x", k))

    def ring_slot(b):
        return b % 8

    def attention_prompt(T, b0, main_tile0, LA=2, heads=None, after_item=None):
        nb = T // 128
        items = []
        for h in (range(NH) if heads is None else heads):
            hitems = []
            for kb in range(b0 - 4, b0 + nb):
                if kb < -5:
                    continue
                c0 = max(2 * kb, 2 * b0)
                c1 = min(2 * kb + 9, 2 * (b0 + nb) - 1)
                if c0 > c1:
                    continue
                hitems.append(dict(h=h, kb=kb, q0=64 * (c0 - 2 * b0), nq=64 * (c1 - c0 + 1), m0=64 * c0 - 128 * kb,
                                   sl=ring_slot(kb), first=False, last=False))
            hitems[0]["first"] = True
            hitems[-1]["last"] = True
            items += hitems

        def front(i, it):
            h, sl, q0, nq, m0 = it["h"], it["sl"], it["q0"], it["nq"], it["m0"]
            sb_ = 512 * (i % 2)
            S.op("pe", lambda e: e.matmul(pS[:, sb_:sb_ + nq], KT[:, h, sl * 128:(sl + 1) * 128], act[:, h, q0:q0 + nq],
                                          start=True, stop=True),
                 reads=[("KT", h, sl), ("act", h)], writes=[("pS", sb_)])
            t = next_tmp()
            S.op("dve", lambda e: e.scalar_tensor_tensor(tmpf[:, t, 0:nq], pS[:, sb_:sb_ + nq], 60.0, TB[:, h, m0:m0 + nq],
                                                         ALU.min, ALU.add),
                 reads=[("pS", sb_), "TB"], writes=[("tmpf", t)])
            p = next_pb()
            S.op("act", lambda e: e.activation(pbf[:, p, 0:nq], tmpf[:, t, 0:nq], AF.Exp),
                 reads=[("tmpf", t)], writes=[("pb", p)])
            it["p"] = p

        def back(it):
            h, sl, q0, nq, p, first, kb = it["h"], it["sl"], it["q0"], it["nq"], it["p"], it["first"], it["kb"]
            po, pd, ko, kd = pO, pD, "pO", "pD"
            S.op("pe", lambda e: e.matmul(po[:, q0:q0 + nq], VR[:, sl, h * 128:(h + 1) * 128], pbf[:, p, 0:nq],
                                          start=first, stop=False, skip_group_check=True),
                 reads=[("VR", sl, h // 2), ("pb", p)], writes=[ko])
            use_h = main_tile0 and kb < 0
            ones_ap = honesb if use_h else onesb
            S.op("pe", lambda e: e.matmul(pd[:, q0:q0 + nq], ones_ap[:], pbf[:, p, 0:nq], start=first, stop=False,
                                          skip_group_check=True),
                 reads=["honesb" if use_h else "onesb", ("pb", p)], writes=[kd])
            if it["last"]:
                t = next_tmp()
                t2 = next_tmp()
                S.op("act", lambda e: e.activation(tmpf[:, t, 0:T], pd[:, 0:T], AF.Ln), reads=[kd], writes=[("tmpf", t)])
                S.op("act", lambda e: e.copy(tmpf[:, t2, 0:T], po[:, 0:T]), reads=[ko], writes=[("tmpf", t2)])
                S.op("act", lambda e: e.activation(tmpf[:, t, 0:T], tmpf[:, t, 0:T], AF.Exp, scale=-1.0),
                     reads=[("tmpf", t)], writes=[("tmpf", t)])
                S.op("pool", lambda e: e.tensor_tensor(act[:, 24 + h, 0:T], tmpf[:, t2, 0:T], tmpf[:, t, 0:T], ALU.mult),
                     reads=[("tmpf", t2), ("tmpf", t)], writes=[("act", 24 + h)])

        n = len(items)
        for i in range(min(LA, n)):
            front(i, items[i])
        for i in range(n):
            back(items[i])
            if i + LA < n:
                front(i + LA, items[i + LA])
            if after_item is not None:
                after_item()

    def sgu_prompt(T):
        nb = T // 128
        for g in range(NH):
            b = next_mm()
            for j in range(nb):
                S.op("pe", lambda e, g=g, j=j, b=b: e.matmul(pmm[b][:, j * 128:(j + 1) * 128],
                                                             act[:, 16 + 2 * j + g // 4, (g % 4) * 128:(g % 4 + 1) * 128],
                                                             wsT[:, g, :], start=True, stop=True),
                     reads=[("act", 16 + 2 * j + g // 4), "wsT"], writes=[("pmm", b)])
            t = next_tmp()
            S.op("dve", lambda e, g=g, b=b, t=t: e.tensor_tensor(
                tmpf[:, t, 0:T].rearrange("p (j i) -> p j i", i=128),
                pmm[b][:, 0:T].rearrange("p (j i) -> p j i", i=128),
                BS[:, g, :].unsqueeze(1).to_broadcast([128, nb, 128]), ALU.add),
                reads=[("pmm", b), "BS"], writes=[("tmpf", t)])
            S.op("dve", lambda e, g=g, t=t: e.tensor_tensor(act[:, g, 0:T], tmpf[:, t, 0:T], act[:, 8 + g, 0:T], ALU.mult),
                 reads=[("tmpf", t), ("act", 8 + g)], writes=[("act", g)])

    def merge(T):
        for f2 in range(0, 16, 2):
            wga, kga = wload(w_in_s, "in2", 0, 16, 5120 + f2 * 128, 5120 + (f2 + 2) * 128)
            wa, ka = wload(w_a_s, "a", 0, 8, f2 * 128, (f2 + 2) * 128)
            wgb, kgb = wload(w_in_s, "in2", 0, 16, 7168 + f2 * 128, 7168 + (f2 + 2) * 128)
            wb_, kb_ = wload(w_b_s, "b", 0, 8, f2 * 128, (f2 + 2) * 128)
            for cc in range(2):
                f = f2 + cc
                b1 = next_mm()
                for k in range(16):
                    S.op("pe", lambda e, k=k, cc=cc, b1=b1, wga=wga: e.matmul(pmm[b1][:, 0:T], wga[:, k, cc * 128:(cc + 1) * 128],
                                                                             xn[:, k, 0:T], start=(k == 0), stop=(k == 15)),
                         reads=[kga, ("xn", k)], writes=[("pmm", b1)])
                t1 = next_tmp()
                S.op("act", lambda e, b1=b1, t1=t1: e.activation(tmpf[:, t1, 0:T], pmm[b1][:, 0:T], AF.Sigmoid),
                     reads=[("pmm", b1)], writes=[("tmpf", t1)])
                b2 = next_mm()
                for k in range(8):
                    S.op("pe", lambda e, k=k, cc=cc, b2=b2, wa=wa: e.matmul(pmm[b2][:, 0:T], wa[:, k, cc * 128:(cc + 1) * 128],
                                                                           act[:, 24 + k, 0:T], start=(k == 0), stop=(k == 7)),
                         reads=[ka, ("act", 24 + k)], writes=[("pmm", b2)])
                S.op("dve", lambda e, b2=b2, t1=t1: e.tensor_tensor(tmpf[:, t1, 0:T], pmm[b2][:, 0:T], tmpf[:, t1, 0:T], ALU.mult),
                     reads=[("pmm", b2), ("tmpf", t1)], writes=[("tmpf", t1)])
                b3 = next_mm()
                for k in range(16):
                    S.op("pe", lambda e, k=k, cc=cc, b3=b3, wgb=wgb: e.matmul(pmm[b3][:, 0:T], wgb[:, k, cc * 128:(cc + 1) * 128],
                                                                             xn[:, k, 0:T], start=(k == 0), stop=(k == 15)),
                         reads=[kgb, ("xn", k)], writes=[("pmm", b3)])
                t2 = next_tmp()
                S.op("act", lambda e, b3=b3, t2=t2: e.activation(tmpf[:, t2, 0:T], pmm[b3][:, 0:T], AF.Sigmoid),
                     reads=[("pmm", b3)], writes=[("tmpf", t2)])
                b4 = next_mm()
                for k in range(8):
                    S.op("pe", lambda e, k=k, cc=cc, b4=b4, wb_=wb_: e.matmul(pmm[b4][:, 0:T], wb_[:, k, cc * 128:(cc + 1) * 128],
                                                                             act[:, k, 0:T], start=(k == 0), stop=(k == 7)),
                         reads=[kb_, ("act", k)], writes=[("pmm", b4)])
                S.op("dve", lambda e, b4=b4, t2=t2: e.tensor_tensor(tmpf[:, t2, 0:T], pmm[b4][:, 0:T], tmpf[:, t2, 0:T], ALU.mult),
                     reads=[("pmm", b4), ("tmpf", t2)], writes=[("tmpf", t2)])
                S.op("dve", lambda e, f=f, t1=t1, t2=t2: e.tensor_tensor(act[:, 8 + f, 0:T], tmpf[:, t1, 0:T], tmpf[:, t2, 0:T], ALU.add),
                     reads=[("tmpf", t1), ("tmpf", t2)], writes=[("act", 8 + f)])

    def out_proj(T):
        stt = Stats(T, lag=2)

        def evac(c, b):
            S.op("dve", lambda e, c=c, b=b: e.tensor_tensor(xT[:, c, 0:T], pmm[b][:, 0:T], xT[:, c, 0:T], ALU.add),
                 reads=[("pmm", b), ("x", c)], writes=[("x", c)])
            stt.chunk(c)
        proj_fm(T, w_out_s, "out", 16, 0, 16, lambda k: act[:, 8 + k, 0:T], [("act", 8 + k) for k in range(16)], evac)
        stt.end()

    def ffn(T, do_down, conv_mode, last2=None, final_stats=False, pre_hist=False):
        passes = [(0, 6), (6, 11)] if do_down else [(0, 11)]
        stt = None
        for (ga0, ga1) in passes:
            nslot = 0
            for ga in range(ga0, ga1):
                for c2 in range(0, 4, 2):
                    wls = []
                    for part in range(2):
                        col0 = (part * 44 + ga * 4) * 128
                        wls.append(wload(w_up_s, "up", 0, 16, col0 + c2 * 128, col0 + (c2 + 2) * 128))
                    for cc in range(2):
                        ci = c2 + cc
                        tg = None
                        for part in range(2):
                            wv, wk = wls[part]
                            c = part * 44 + ga * 4 + ci
                            if pre_hist:
                                b2 = next_mm()
                                for k in range(16):
                                    S.op("pe", lambda e, k=k, cc=cc, b2=b2, wv=wv: e.matmul(pmm[b2][:, 0:2], wv[:, k, cc * 128:(cc + 1) * 128],
                                                                                           xh2[:, k, :], start=(k == 0), stop=(k == 15)),
                                         reads=[wk, "xh2"], writes=[("pmm", b2)])
                                S.op("act", lambda e, c=c, b2=b2: e.copy(hist[:, c, :], pmm[b2][:, 0:2]),
                                     reads=[("pmm", b2)], writes=[("hist", c)])
                            b = next_mm()
                            for k in range(16):
                                S.op("pe", lambda e, k=k, cc=cc, b=b, wv=wv: e.matmul(pmm[b][:, 0:T], wv[:, k, cc * 128:(cc + 1) * 128],
                                                                                     xn[:, k, 0:T], start=(k == 0), stop=(k == 15)),
                                     reads=[wk, ("xn", k)], writes=[("pmm", b)])
                            th = next_tmp()
                            if conv_mode == "prompt":
                                S.op("act", lambda e, b=b, th=th: e.copy(tmpf[:, th, 2:2 + T], pmm[b][:, 0:T]),
                                     reads=[("pmm", b)], writes=[("tmpf", th)])
                                S.op("act", lambda e, c=c, th=th: e.copy(tmpf[:, th, 0:2], hist[:, c, :]),
                                     reads=[("hist", c)], writes=[("tmpf", th)])
                                S.op("act", lambda e, c=c, th=th: e.copy(hist[:, c, :], tmpf[:, th, T:T + 2]),
                                     reads=[("tmpf", th)], writes=[("hist", c)])
                                if not do_down:
                                    continue
                                ta = next_tmp()
                                hb = lambda o, th=th: tmpf[:, th, o:o + T]
                                av = lambda ta=ta: tmpf[:, ta, 0:T]
                            else:
                                ta = next_tmp()
                                hv = lambda th=th: tmpf[:, th, 0:72].rearrange("p (s i) -> p s i", i=18)
                                S.op("act", lambda e, b=b, hv=hv: e.copy(hv()[:, :, 2:18], pmm[b][:, 0:64].rearrange("p (s i) -> p s i", i=16)),
                                     reads=[("pmm", b)], writes=[("tmpf", th)])
                                S.op("act", lambda e, c=c, hv=hv: e.copy(hv()[:, :, 0:2], hs_hist[:, c, :, :]),
                                     reads=[("hs_hist", (c // 11) * 11)], writes=[("tmpf", th)])
                                S.op("act", lambda e, c=c, hv=hv: e.copy(hs_last[:, c, :, :], hv()[:, :, 16:18]),
                                     reads=[("tmpf", th)], writes=["hs_last"])
                                hb = lambda o, hv=hv: hv()[:, :, o:o + 16]
                                av = lambda ta=ta: tmpf[:, ta, 0:64].rearrange("p (s i) -> p s i", i=16)
                            S.op("dve", lambda e, c=c, hb=hb, av=av: e.tensor_scalar(av(), hb(2), convp[:, c, 2:3], convp[:, c, 3:4],
                                                                                   ALU.mult, ALU.add),
                                 reads=[("tmpf", th), "convp"], writes=[("tmpf", ta)])
                            S.op("dve", lambda e, c=c, hb=hb, av=av: e.scalar_tensor_tensor(av(), hb(1), convp[:, c, 1:2], av(),
                                                                                          ALU.mult, ALU.add),
                                 reads=[("tmpf", th), ("tmpf", ta), "convp"], writes=[("tmpf", ta)])
                            S.op("dve", lambda e, c=c, hb=hb, av=av: e.scalar_tensor_tensor(av(), hb(0), convp[:, c, 0:1], av(),
                                                                                          ALU.mult, ALU.add),
                                 reads=[("tmpf", th), ("tmpf", ta), "convp"], writes=[("tmpf", ta)])
                            if part == 0:
                                S.op("act", lambda e, ta=ta: e.activation(tmpf[:, ta, 0:T], tmpf[:, ta, 0:T], AF.Gelu),
                                     reads=[("tmpf", ta)], writes=[("tmpf", ta)])
                                tg = ta
                            else:
                                slot = (ga - ga0) * 4 + ci
                                S.op("dve", lambda e, tg=tg, ta=ta, slot=slot: e.tensor_tensor(act[:, slot, 0:T], tmpf[:, tg, 0:T],
                                                                                              tmpf[:, ta, 0:T], ALU.mult),
                                     reads=[("tmpf", tg), ("tmpf", ta)], writes=[("act", slot)])
            if not do_down:
                continue
            nch = (ga1 - ga0) * 4
            kc0 = ga0 * 4
            halves = [(0, nch // 2), (nch // 2, nch)]
            for f2 in range(0, 16, 2):
                banks = [next_mm(), next_mm()]
                for hi, (s0, s1) in enumerate(halves):
                    wv, wk = wload(w_down_s, "down", kc0 + s0, kc0 + s1, f2 * 128, (f2 + 2) * 128)
                    for cc in range(2):
                        for s in range(s0, s1):
                            S.op("pe", lambda e, s=s, s0=s0, cc=cc, wv=wv, bb=banks[cc]: e.matmul(
                                pmm[bb][:, 0:T], wv[:, s - s0, cc * 128:(cc + 1) * 128], act[:, s, 0:T],
                                start=(s == 0), stop=(s == nch - 1), skip_group_check=True),
                                reads=[wk, ("act", s)], writes=[("pmm", banks[cc])])
                if final_stats and ga1 == 11 and stt is None:
                    stt = Stats(T, lag=2)
                for cc in range(2):
                    f = f2 + cc
                    S.op("dve", lambda e, f=f, bb=banks[cc]: e.tensor_tensor(xT[:, f, 0:T], pmm[bb][:, 0:T], xT[:, f, 0:T], ALU.add),
                         reads=[("pmm", banks[cc]), ("x", f)], writes=[("x", f)])
                    if stt is not None:
                        stt.chunk(f)
            if stt is not None:
                stt.end()

    def final_out(T, out_rows, have_stats=True, next_x=None):
        if not have_stats:
            rms_stats(T, None)
        nbk = max(1, T // 128)
        ntok = min(T, 128)
        for kg in range(4):
            ts = [0, 1, 2, 3]
            for kk in range(4):
                k = kg * 4 + kk
                t = ts[kk]
                S.op("dve", lambda e, k=k, t=t: e.scalar_tensor_tensor(tmpf[:, t, 0:T], xT[:, k, 0:T], gcols[:, 32 + k:33 + k],
                                                                      rinv[:, 0:T], ALU.mult, ALU.mult),
                     reads=[("x", k), "gcols", "rinv"], writes=[("tmpf", t)])
            if next_x is not None:
                for kk in range(4):
                    k = kg * 4 + kk
                    S.dma("sp", xT[:, k, 0:512], next_x[k * 128:(k + 1) * 128, :], reads=(), writes=[("x", k)], sem=("x", k))
            for j in range(nbk):
                b = next_mm()
                for kk in range(4):
                    S.op("pe", lambda e, kk=kk, b=b, j=j, t=ts[kk]: e.transpose(pmm[b][0:ntok, kk * 128:(kk + 1) * 128],
                                                                              tmpf[:, t, j * 128:j * 128 + ntok], ident[:]),
                         reads=[("tmpf", ts[kk]), "ident"], writes=[("pmm", b)])
                to = 4 + (st["fo"] % 2)
                st["fo"] += 1
                S.op("act", lambda e, b=b, to=to: e.copy(tmpf[0:ntok, to, 0:512], pmm[b][0:ntok, 0:512]),
                     reads=[("pmm", b)], writes=[("tmpf", to)])
                S.dma("sp" if next_x is not None else "act", out_rows[j * 128:j * 128 + ntok, kg * 512:(kg + 1) * 512],
                      tmpf[0:ntok, to, 0:512], reads=[("tmpf", to)], writes=(), sem=("tmpf", to))

    def make_qkv_evacs(T, b0, out_kv):
        nb = T // 128
        slot0 = ring_slot(b0)

        def evq(c, b):
            S.op("act", lambda e, c=c, b=b: e.activation(act[:, c, 0:T], pmm[b][:, 0:T], AF.Copy, scale=float(128 ** -0.5)),
                 reads=[("pmm", b)], writes=[("act", c)])

        def evk(c, b):
            eng = ev_eng()
            keys = [("KT", c, slot0 + j) for j in range(nb)]
            if eng == "act":
                S.op("act", lambda e, c=c, b=b: e.copy(KT[:, c, slot0 * 128:slot0 * 128 + T], pmm[b][:, 0:T]),
                     reads=[("pmm", b)], writes=keys)
            else:
                S.op("dve", lambda e, c=c, b=b: e.tensor_copy(KT[:, c, slot0 * 128:slot0 * 128 + T], pmm[b][:, 0:T]),
                     reads=[("pmm", b)], writes=keys)

        def evv(j, g, b):
            sl = slot0 + j
            eng = "act" if out_kv else ev_eng()
            if eng == "act":
                S.op("act", lambda e, sl=sl, g=g, b=b: e.copy(VR[:, sl, g * 256:(g + 1) * 256], pmm[b][:, 0:256]),
                     reads=[("pmm", b)], writes=[("VR", sl, g)])
            else:
                S.op("dve", lambda e, sl=sl, g=g, b=b: e.tensor_copy(VR[:, sl, g * 256:(g + 1) * 256], pmm[b][:, 0:256]),
                     reads=[("pmm", b)], writes=[("VR", sl, g)])
            if out_kv:
                to = next_tmp()
                S.op("act", lambda e, b=b, to=to: e.copy(tmpf[:, to, 0:256], pmm[b][:, 0:256]),
                     reads=[("pmm", b)], writes=[("tmpf", to)])
                S.dma("act", nv_d[j * 128:(j + 1) * 128, g * 256:(g + 1) * 256], tmpf[:, to, 0:256],
                      reads=[("tmpf", to)], writes=(), sem=("tmpf", to))
        return evq, evk, evv

    def qkv_fillers(T, b0, out_kv):
        nb = T // 128
        evq, evk, evv = make_qkv_evacs(T, b0, out_kv)
        for pr in range(4):
            wq = wload(w_in_s, "in0", 0, 16, pr * 256, pr * 256 + 256)
            wkk = wload(w_in_s, "in0", 0, 16, 1024 + pr * 256, 1024 + pr * 256 + 256)
            for cc in range(2):
                for ((wv, wk), ev) in ((wq, evq), (wkk, evk)):
                    b = next_mm()
                    for k in range(16):
                        S.op("pe", lambda e, k=k, cc=cc, b=b, wv=wv: e.matmul(pmm[b][:, 0:T], wv[:, k, cc * 128:(cc + 1) * 128],
                                                                             xn[:, k, 0:T], start=(k == 0), stop=(k == 15)),
                             reads=[wk, ("xn", k)], writes=[("pmm", b)])
                        if k % 4 == 3 and k != 15:
                            yield None
                    ev(2 * pr + cc, b)
                    yield None
                if cc == 0:
                    wv, wk = wload(w_in_s, "in0", 0, 16, 2048 + pr * 256, 2048 + (pr + 1) * 256)
                    for j in range(nb):
                        b = next_mm()
                        for k in range(16):
                            S.op("pe", lambda e, k=k, b=b, wv=wv, j=j: e.matmul(pmm[b][0:128, 0:256], xn[:, k, j * 128:(j + 1) * 128],
                                                                               wv[:, k, :], start=(k == 0), stop=(k == 15)),
                                 reads=[wk, ("xn", k)], writes=[("pmm", b)])
                            if k % 4 == 3 and k != 15:
                                yield None
                        evv(j, pr, b)
                        yield None
                yield "head"

    def qkv_attention(T, b0, main_tile0, out_kv):
        gen = qkv_fillers(T, b0, out_kv)

        def drain_head():
            while True:
                r = next(gen, "end")
                if r in ("head", "end"):
                    return

        drain_head()
        for h in range(NH):
            state = {"done": h == NH - 1}
            npull = 3 if (h + 1) % 2 == 0 else 1

            def after_item():
                if state["done"]:
                    return
                for _ in range(npull):
                    r = next(gen, "end")
                    if r in ("head", "end"):
                        state["done"] = True
                        return
            attention_prompt(T, b0, main_tile0, heads=(h,), after_item=after_item)
            if not state["done"]:
                drain_head()

    def mixer_inputs_prompt(T, b0, kv_only, out_kv, main_tile0=False):
        nb = T // 128
        rms_stats(T, None)
        normalize_to_xn(T, 0)
        if kv_only:
            evq, evk, evv = make_qkv_evacs(T, b0, out_kv)
            proj_fm(T, w_in_s, "in0", 16, 1024, 8, lambda k: xn[:, k, 0:T], XN_ALL, evk)
            proj_tm(T, nb, 2048, evv, "in0")
            return

        vb_stage(T, [(j * 128, 128) for j in range(nb)], None)

        def evu(c, b):
            S.op("act", lambda e, c=c, b=b: e.activation(act[:, 8 + c, 0:T], pmm[b][:, 0:T], AF.Gelu),
                 reads=[("pmm", b)], writes=[("act", 8 + c)])
        proj_fm(T, w_in_s, "in1", 16, 3072, 8, lambda k: xn[:, k, 0:T], XN_ALL, evu)

        qkv_attention(T, b0, main_tile0, out_kv)
        if out_kv:
            def evko(j, g, b):
                to = next_tmp()
                S.op("act", lambda e, b=b, to=to: e.copy(tmpf[:, to, 0:256], pmm[b][:, 0:256]),
                     reads=[("pmm", b)], writes=[("tmpf", to)])
                S.dma("act", nk_d[j * 128:(j + 1) * 128, g * 256:(g + 1) * 256], tmpf[:, to, 0:256],
                      reads=[("tmpf", to)], writes=(), sem=("tmpf", to))
            proj_tm(T, nb, 1024, evko, "in0")

    def vb_stage(T, tok_slices, out_dram):
        tmps = {}

        def evvb(j, g, b):
            nt = tok_slices[j][1]
            if g == 0:
                tmps[j] = (next_tmp(), next_tmp())
            t = tmps[j][g // 2]
            o = (g % 2) * 256
            S.op("act", lambda e, b=b, t=t, o=o, nt=nt: e.activation(tmpf[0:nt, t, o:o + 256], pmm[b][0:nt, 0:256], AF.Gelu),
                 reads=[("pmm", b)], writes=[("tmpf", t)])
            if g % 2 == 1:
                tj = next_tmp()
                S.op("dve", lambda e, t=t, tj=tj, g=g, nt=nt: e.scalar_tensor_tensor(
                    tmpf[0:nt, tj, 0:512], tmpf[0:nt, t, 0:512], 1.0, tmpf[0:nt, t, 0:512], ALU.mult, ALU.mult,
                    accum_out=cols[0:nt, 1 + g // 2:2 + g // 2]),
                    reads=[("tmpf", t)], writes=[("tmpf", tj), ("cols", 1 + g // 2)])
            if g == 3:
                ta, tb = tmps[j]
                S.op("dve", lambda e, nt=nt: e.tensor_tensor(cols[0:nt, 3:4], cols[0:nt, 1:2], cols[0:nt, 2:3], ALU.add),
                     reads=[("cols", 1), ("cols", 2)], writes=[("cols", 3)])
                S.op("act", lambda e, nt=nt: e.activation(cols[0:nt, 4:5], cols[0:nt, 3:4], AF.Sqrt, bias=cols[0:nt, 0:1], scale=1.0 / 1024),
                     reads=[("cols", 3), "cols"], writes=[("cols", 4)])
                S.op("dve", lambda e, nt=nt: e.reciprocal(cols[0:nt, 5:6], cols[0:nt, 4:5]), reads=[("cols", 4)], writes=[("cols", 5)])
                for hh, t in enumerate((ta, tb)):
                    if out_dram is None:
                        dst = act[0:nt, 16 + 2 * j + hh, 0:512]
                        S.op("dve", lambda e, t=t, hh=hh, dst=dst, nt=nt: e.scalar_tensor_tensor(
                            dst, tmpf[0:nt, t, 0:512], cols[0:nt, 5:6], gsgu[0:nt, hh * 512:(hh + 1) * 512], ALU.mult, ALU.mult),
                            reads=[("tmpf", t), ("cols", 5), "gsgu"], writes=[("act", 16 + 2 * j + hh)])
                    else:
                        S.op("dve", lambda e, t=t, hh=hh, nt=nt: e.scalar_tensor_tensor(
                            tmpf[0:nt, t, 0:512], tmpf[0:nt, t, 0:512], cols[0:nt, 5:6], gsgu[0:nt, hh * 512:(hh + 1) * 512],
                            ALU.mult, ALU.mult),
                            reads=[("tmpf", t), ("cols", 5), "gsgu"], writes=[("tmpf", t)])
                        S.op("act", lambda e, t=t, hh=hh, nt=nt: e.copy(act[0:nt, 16 + hh, 0:512], tmpf[0:nt, t, 0:512]),
                             reads=[("tmpf", t)], writes=[("act", 16 + hh)])
                        S.dma("act", out_dram[0:nt, hh * 512:(hh + 1) * 512], tmpf[0:nt, t, 0:512],
                              reads=[("tmpf", t)], writes=(), sem=("tmpf", t))
        proj_tm(T, len(tok_slices), 4096, evvb, "in1", tok_slices)

    S.op("pool", lambda e: e.memset(cols[:, 0:1], EPS), writes=["cols"])
    xTr = xT_d

    def prompt_tile(col0, T, b0, kv_only=False, upto_h=False, main_idx=None):
        if not (main_idx is not None and main_idx >= 1):
            load_x(xTr[:, col0:col0 + T], T)
        last = (main_idx == 7)
        mixer_inputs_prompt(T, b0, kv_only, out_kv=last, main_tile0=(main_idx == 0))
        if kv_only:
            return
        sgu_prompt(T)
        merge(T)
        out_proj(T)
        normalize_to_xn(T, 16)
        if upto_h:
            S.op("act", lambda e: e.copy(xh2[:], xn[:, :, T - 2:T]), reads=XN_ALL, writes=["xh2"])
            return
        ffn(T, do_down=True, conv_mode="prompt", final_stats=True, pre_hist=(main_idx == 0 and do_halo))
        nxt = None
        if main_idx + 1 < n_main:
            c1 = HALO + 512 * (main_idx + 1)
            nxt = xTr[:, c1:c1 + 512]
        final_out(T, y_d[main_idx * 512:(main_idx + 1) * 512, :], next_x=nxt)

    if do_halo:
        prompt_tile(0, 512, -5, kv_only=True)
        prompt_tile(512, 128, -1, upto_h=True)
    for i in range(n_main):
        prompt_tile(HALO + 512 * i, 512, 4 * i, main_idx=i)
    if n_main == 8:
        for t2 in range(2):
            b = next_mm()
            S.op("pe", lambda e, b=b, t2=t2: e.transpose(pmm[b][0:88, 0:128], hist[:, :, t2], ident[:]),
                 reads=[("hist", c) for c in range(88)] + ["ident"], writes=[("pmm", b)])
            to = next_tmp()
            S.op("act", lambda e, b=b, to=to: e.copy(tmpf[0:88, to, 0:128], pmm[b][0:88, 0:128]),
                 reads=[("pmm", b)], writes=[("tmpf", to)])
            S.dma("act", ncp_d[t2].rearrange("(c p) -> c p", p=128), tmpf[0:88, to, 0:128],
                  reads=[("tmpf", to)], writes=(), sem=("tmpf", to))

    if do_sample:
        sample_tile(S, nc, locals())

    S.finish()
    return nc


def sample_tile(S, nc, L):
    g = L
    xT, xn, KT, VR, act, TB, tmpf, pbf = g["xT"], g["xn"], g["KT"], g["VR"], g["act"], g["TB"], g["tmpf"], g["pbf"]
    pmm, pS, pO, pD = g["pmm"], g["pS"], g["pO"], g["pD"]
    next_mm, next_tmp, next_pb, wload = g["next_mm"], g["next_tmp"], g["next_pb"], g["wload"]
    kTs, wbd, BS, onesb, hs_hist, hs_last, ident = g["kTs"], g["wbd"], g["BS"], g["onesb"], g["hs_hist"], g["hs_last"], g["ident"]
    w_in_s = g["w_in_s"]
    XN_ALL = g["XN_ALL"]
    T = 64
    for k in range(16):
        S.dma("sp", xT[:, k, 0:T], g["xsT_d"][k * 128:(k + 1) * 128, :], reads=(), writes=[("x", k)], sem=("x", k))
    cview = g["cconvT_d"].rearrange("(c p) (s t) -> p c s t", p=128, t=2)
    for c8 in range(0, 88, 11):
        S.dma("sp", hs_hist[:, c8:c8 + 11], cview[:, c8:c8 + 11], reads=(), writes=[("hs_hist", c8)], sem="hs_hist", total=True)
    g["rms_stats"](T, None)
    g["normalize_to_xn"](T, 0)

    def evq(c, b):
        S.op("act", lambda e, c=c, b=b: e.activation(act[:, c, 0:T], pmm[b][:, 0:T], AF.Copy, scale=float(128 ** -0.5)),
             reads=[("pmm", b)], writes=[("act", c)])
    g["proj_fm"](T, w_in_s, "in0", 16, 0, 8, lambda k: xn[:, k, 0:T], XN_ALL, evq)

    def evk(c, b):
        S.op("act", lambda e, c=c, b=b: e.copy(kTs[:, c, :], pmm[b][:, 0:T]), reads=[("pmm", b)], writes=["kTs"])
    g["proj_fm"](T, w_in_s, "in0", 16, 1024, 8, lambda k: xn[:, k, 0:T], XN_ALL, evk)

    def evko(j, gg, b):
        to = next_tmp()
        S.op("act", lambda e, b=b, to=to: e.copy(tmpf[0:64, to, 0:256], pmm[b][0:64, 0:256]), reads=[("pmm", b)], writes=[("tmpf", to)])
        S.dma("act", g["nks_d"][:, gg * 256:(gg + 1) * 256], tmpf[0:64, to, 0:256], reads=[("tmpf", to)], writes=(), sem=("tmpf", to))
    g["proj_tm"](T, 1, 1024, evko, "in0", [(0, 64)])

    def evv(j, gg, b):
        s = j
        S.op("act", lambda e, s=s, gg=gg, b=b: e.copy(VR[0:16, 4 + s, gg * 256:(gg + 1) * 256], pmm[b][0:16, 0:256]),
             reads=[("pmm", b)], writes=[("VR", 4 + s, gg)])
        to = next_tmp()
        S.op("act", lambda e, b=b, to=to: e.copy(tmpf[0:16, to, 0:256], pmm[b][0:16, 0:256]), reads=[("pmm", b)], writes=[("tmpf", to)])
        S.dma("act", g["nvs_d"][16 * s:16 * s + 16, gg * 256:(gg + 1) * 256], tmpf[0:16, to, 0:256],
              reads=[("tmpf", to)], writes=(), sem=("tmpf", to))
    g["proj_tm"](T, 4, 2048, evv, "in0", [(16 * s, 16) for s in range(4)])

    def evu(c, b):
        S.op("act", lambda e, c=c, b=b: e.activation(act[:, 8 + c, 0:T], pmm[b][:, 0:T], AF.Gelu), reads=[("pmm", b)], writes=[("act", 8 + c)])
    g["proj_fm"](T, w_in_s, "in1", 16, 3072, 8, lambda k: xn[:, k, 0:T], XN_ALL, evu)
    g["vb_stage"](T, [(0, 64)], g["nvbs_d"])

    first = True
    for s in range(4):
        half = (s % 2) * 512
        kkeys = [("KT", h, 4 * (s % 2) + j) for h in range(8) for j in range(4)]
        S.dma("pool", KT[:, :, half:half + 512], g["ckT_d"][s].rearrange("h d k -> d h k"), reads=(), writes=kkeys,
              sem=("KTc", s % 2))
        vkeys = [("VR", j, gq) for j in range(4) for gq in range(4)]
        S.dma("pool", VR[:, 0:4, :], g["cv_d"][s].rearrange("(b p) f -> p b f", p=128), reads=(), writes=vkeys, sem="VRc")
        for h in range(8):
            for kt in range(4):
                cb = (h * 5 + kt) * 16
                S.op("pe", lambda e, h=h, kt=kt, cb=cb, s=s, half=half: e.matmul(
                    pS[:, cb:cb + 16], KT[:, h, half + kt * 128:half + (kt + 1) * 128], act[:, h, 16 * s:16 * s + 16],
                    start=True, stop=True),
                    reads=[("KT", h, 4 * (s % 2) + kt), ("act", h)], writes=[("pS", 0), ("pS", 512)])
            cb = (h * 5 + 4) * 16
            S.op("pe", lambda e, h=h, cb=cb, s=s: e.matmul(
                pS[0:16, cb:cb + 16], kTs[:, h, 16 * s:16 * s + 16], act[:, h, 16 * s:16 * s + 16], start=True, stop=True),
                reads=["kTs", ("act", h)], writes=[("pS", 0), ("pS", 512)])
        t0, t1 = next_tmp(), next_tmp()

        def sview(c):
            return (t0, c) if c < 512 else (t1, c - 512)
        for h in range(8):
            for kt in range(5):
                cb = (h * 5 + kt) * 16
                m0 = 512 - 128 * kt if kt < 4 else 0
                ti, off = sview(cb)
                np_ = 128 if kt < 4 else 16
                S.op("dve", lambda e, h=h, cb=cb, m0=m0, ti=ti, off=off, np_=np_: e.scalar_tensor_tensor(
                    tmpf[0:np_, ti, off:off + 16], pS[0:np_, cb:cb + 16], 60.0, TB[0:np_, h, m0:m0 + 16], ALU.min, ALU.add),
                    reads=[("pS", 0), ("pS", 512), "TB"], writes=[("tmpf", ti)])
        p = next_pb()
        S.op("act", lambda e, p=p, t0=t0: e.activation(pbf[:, p, 0:512], tmpf[:, t0, 0:512], AF.Exp),
             reads=[("tmpf", t0)], writes=[("pb", p)])
        S.op("act", lambda e, p=p, t1=t1: e.activation(pbf[:, p, 512:640], tmpf[:, t1, 0:128], AF.Exp),
             reads=[("tmpf", t1)], writes=[("pb", p)])
        for h in range(8):
            ob = (s * 8 + h) * 16
            for kt in range(5):
                cb = (h * 5 + kt) * 16
                if kt < 4:
                    lv = VR[:, kt, h * 128:(h + 1) * 128]
                    lo = onesb[:]
                    rp = pbf[:, p, cb:cb + 16]
                    rk = [("VR", kt, h // 2)]
                else:
                    lv = VR[0:16, 4 + s, h * 128:(h + 1) * 128]
                    lo = onesb[0:16, :]
                    rp = pbf[0:16, p, cb:cb + 16]
                    rk = [("VR", 4 + s, h // 2)]
                S.op("pe", lambda e, lv=lv, rp=rp, ob=ob, first=first: e.matmul(pO[:, ob:ob + 16], lv, rp, start=first, stop=False,
                                                                               skip_group_check=True),
                     reads=rk + [("pb", p)], writes=["pO"])
                S.op("pe", lambda e, lo=lo, rp=rp, ob=ob, first=first: e.matmul(pD[:, ob:ob + 16], lo, rp, start=first, stop=False,
                                                                               skip_group_check=True),
                     reads=["onesb", ("pb", p)], writes=["pD"])
                first = False
    t = next_tmp()
    S.op("dve", lambda e, t=t: e.reciprocal(tmpf[:, t, 0:512], pD[:, 0:512]), reads=["pD"], writes=[("tmpf", t)])
    S.op("dve", lambda e, t=t: e.tensor_tensor(tmpf[:, t, 0:512], pO[:, 0:512], tmpf[:, t, 0:512], ALU.mult),
         reads=["pO", ("tmpf", t)], writes=[("tmpf", t)])
    for h in range(8):
        S.op("act", lambda e, t=t, h=h: e.copy(act[:, 24 + h, 0:64].rearrange("p (s i) -> p s i", i=16),
                                              tmpf[:, t, 0:512].rearrange("p (s h i) -> p h s i", s=4, h=8)[:, h, :, :]),
             reads=[("tmpf", t)], writes=[("act", 24 + h)])

    for gg in range(8):
        b = next_mm()
        S.op("pe", lambda e, gg=gg, b=b: e.matmul(pmm[b][:, 0:64], act[0:64, 16 + gg // 4, (gg % 4) * 128:(gg % 4 + 1) * 128],
                                                  wbd[:, gg, :], start=True, stop=True),
             reads=[("act", 16 + gg // 4)] + [("wbdl", q) for q in range(4)], writes=[("pmm", b)])
        t = next_tmp()
        S.op("dve", lambda e, gg=gg, b=b, t=t: e.tensor_tensor(
            tmpf[:, t, 0:64].rearrange("p (s i) -> p s i", i=16), pmm[b][:, 0:64].rearrange("p (s i) -> p s i", i=16),
            BS[:, gg, 0:16].unsqueeze(1).to_broadcast([128, 4, 16]), ALU.add),
            reads=[("pmm", b), "BS"], writes=[("tmpf", t)])
        S.op("dve", lambda e, gg=gg, t=t: e.tensor_tensor(act[:, gg, 0:64], tmpf[:, t, 0:64], act[:, 8 + gg, 0:64], ALU.mult),
             reads=[("tmpf", t), ("act", 8 + gg)], writes=[("act", gg)])

    g["merge"](T)
    g["out_proj"](T)
    g["normalize_to_xn"](T, 16)
    g["ffn"](T, do_down=True, conv_mode="sample", final_stats=True)
    g["final_out"](T, g["ys_d"])
    for s in range(4):
        for t2 in range(2):
            b = next_mm()
            S.op("pe", lambda e, b=b, s=s, t2=t2: e.transpose(pmm[b][0:88, 0:128], hs_last[:, :, s, t2], ident[:]),
                 reads=["hs_last", "ident"], writes=[("pmm", b)])
            to = next_tmp()
            S.op("act", lambda e, b=b, to=to: e.copy(tmpf[0:88, to, 0:128], pmm[b][0:88, 0:128]), reads=[("pmm", b)], writes=[("tmpf", to)])
            S.dma("act", g["ncs_d"][s, t2].rearrange("(c p) -> c p", p=128), tmpf[0:88, to, 0:128],
                  reads=[("tmpf", to)], writes=(), sem=("tmpf", to))


_NC_CACHE = {}


def _consts(rel_bias):
    r = np.arange(128)[:, None]
    m = np.arange(640)[None, :]
    idx = np.clip(m - r, -256, 256) + 256
    tbg = np.ascontiguousarray(rel_bias[:, idx]).astype(np.float32)
    inval = ((r < 64) & (m >= 576)) | ((r >= 64) & (m < 64))
    mk = np.where(inval, np.float32(NEG), np.float32(0.0)).astype(np.float32)
    triu = np.triu(np.ones((128, 128), np.float32))
    bd = np.zeros((64, 64), np.float32)
    for s in range(4):
        bd[16 * s:16 * s + 16, 16 * s:16 * s + 16] = np.triu(np.ones((16, 16), np.float32))
    return tbg, mk, triu, bd


def make_in_maps(inp, n_cores=8):
    x_prompt = np.asarray(inp["x_prompt"], np.float32)
    x_sample = np.asarray(inp["x_sample"], np.float32)
    cache_k = np.asarray(inp["cache_k"], np.float32)[0]
    cache_v = np.asarray(inp["cache_v"], np.float32)[0]
    cconv = np.asarray(inp["cache_ffn_conv"], np.float32)[0]
    tbg, mk, triu, bd = _consts(np.asarray(inp["rel_bias"], np.float32)[0])
    gcols = np.concatenate([np.asarray(inp[n], np.float32).reshape(16, 128).T
                            for n in ("norm_mix_g", "norm_ffn_g", "norm_final_g")], axis=1)
    conv_w = np.asarray(inp["conv_w"], np.float32)[0]
    conv_b = np.asarray(inp["conv_b"], np.float32)[0]
    cp = np.stack([conv_w[0], conv_w[1], conv_w[2], conv_b], axis=-1)
    convp = np.ascontiguousarray(cp.reshape(88, 128, 4).transpose(1, 0, 2)).reshape(128, 88 * 4)
    shared = {
        "w_in": np.ascontiguousarray(np.asarray(inp["w_in"], np.float32)[0]),
        "w_a": np.ascontiguousarray(np.asarray(inp["w_branch_a"], np.float32)[0]),
        "w_b": np.ascontiguousarray(np.asarray(inp["w_branch_b"], np.float32)[0]),
        "w_out": np.ascontiguousarray(np.asarray(inp["w_out"], np.float32)[0]),
        "w_up": np.ascontiguousarray(np.asarray(inp["w_up"], np.float32)[0]),
        "w_down": np.ascontiguousarray(np.asarray(inp["w_down"], np.float32)[0]),
        "gcols": np.ascontiguousarray(gcols),
        "gsgu": np.ascontiguousarray(np.asarray(inp["sgu_norm_g"], np.float32)[0]),
        "convp": convp,
        "wsT": np.ascontiguousarray(np.asarray(inp["w_s"], np.float32)[0].transpose(0, 2, 1)),
        "bs": np.ascontiguousarray(np.asarray(inp["b_s"], np.float32)[0].reshape(1024)),
        "tbg": tbg, "mk": mk, "ident": np.eye(128, dtype=np.float32), "triu": triu, "maskbd": bd,
    }
    maps = []
    for c in range(n_cores):
        b, half = c // 2, c % 2
        xT = np.zeros((D, HALO + NTOK), np.float32)
        if half == 0:
            xT[:, HALO:] = x_prompt[b, 0:NTOK].T
        else:
            xT[:, :] = x_prompt[b, NTOK - HALO:2 * NTOK].T
        m = dict(shared)
        m["xT"] = xT
        m["hones"] = np.full((128, 128), float(half), np.float32)
        m["xsT"] = np.ascontiguousarray(x_sample[4 * c:4 * c + 4].reshape(64, D).T)
        m["ckT"] = np.ascontiguousarray(cache_k[4 * c:4 * c + 4].transpose(0, 2, 3, 1))
        m["cv"] = np.ascontiguousarray(cache_v[4 * c:4 * c + 4].reshape(4, 512, 1024))
        m["cconvT"] = np.ascontiguousarray(cconv[4 * c:4 * c + 4].transpose(2, 0, 1).reshape(2 * DFF, 8))
        maps.append(m)
    return maps


def assemble(results):
    y_prompt = np.zeros((4, 8192, D), np.float32)
    y_sample = np.zeros((32, 16, D), np.float32)
    nkp = np.zeros((1, 4, 512, 8, 128), np.float32)
    nvp = np.zeros((1, 4, 512, 8, 128), np.float32)
    nks = np.zeros((1, 32, 16, 8, 128), np.float32)
    nvs = np.zeros((1, 32, 16, 8, 128), np.float32)
    nvbs = np.zeros((1, 32, 16, 1024), np.float32)
    ncp = np.zeros((1, 4, 2, 2 * DFF), np.float32)
    ncs = np.zeros((1, 32, 2, 2 * DFF), np.float32)
    for c, r in enumerate(results):
        b, half = c // 2, c % 2
        y_prompt[b, half * NTOK:(half + 1) * NTOK] = r["y"]
        y_sample[4 * c:4 * c + 4] = r["ys"].reshape(4, 16, D)
        if half == 1:
            nkp[0, b] = r["nk"].reshape(512, 8, 128)
            nvp[0, b] = r["nv"].reshape(512, 8, 128)
            ncp[0, b] = r["ncp"]
        nks[0, 4 * c:4 * c + 4] = r["nks"].reshape(4, 16, 8, 128)
        nvs[0, 4 * c:4 * c + 4] = r["nvs"].reshape(4, 16, 8, 128)
        nvbs[0, 4 * c:4 * c + 4] = r["nvbs"].reshape(4, 16, 1024)
        ncs[0, 4 * c:4 * c + 4] = r["ncs"]
    return (y_prompt, y_sample, nkp, nvp, nks, nvs, nvbs, ncp, ncs)


def kernel(**inputs):
    if "nc" not in _NC_CACHE:
        _NC_CACHE["nc"] = build()
    nc = _NC_CACHE["nc"]
    maps = make_in_maps(inputs)
    res = run_bass_kernel_spmd(nc, maps, core_ids=list(range(8)))
    return assemble(res.results)
```
